# Optimizing a Trainium2 kernel written in Bass

```python
import math
import jax, jax.numpy as jnp
from jax import lax
import numpy as np

D_MODEL = 1024
BATCH = 2
SEQ = 8192
DEPTH = 1

D_MIX = D_MODEL
G_WIDTH = D_MIX // 2
R_WIDTH = D_MIX - G_WIDTH
G_HEADS = 4
G_HEAD_DIM = G_WIDTH // G_HEADS
CHUNK = 128
R_HEADS = 8
R_HEAD_DIM = R_WIDTH // R_HEADS
CONV_W = 4
RG_C = 8.0
N_MEM = 256
X_HEADS = 4
X_HEAD_DIM = D_MODEL // X_HEADS
D_FF = int(math.ceil(D_MODEL * 8 / 3 / 256) * 256)
IN_COLS = 2 * G_WIDTH + 2 * R_WIDTH
EPS = 1e-6

kernel_name = "hybrid_gmlp_rglru_xattn_block"


def rms_norm(x, g):
    xf = x.astype(jnp.float32)
    y = xf * lax.rsqrt(jnp.mean(xf * xf, axis=-1, keepdims=True) + EPS)
    return (y * g.astype(jnp.float32)).astype(x.dtype)


def layer_norm(x, g, b):
    xf = x.astype(jnp.float32)
    mu = jnp.mean(xf, axis=-1, keepdims=True)
    var = jnp.mean(jnp.square(xf - mu), axis=-1, keepdims=True)
    y = (xf - mu) * lax.rsqrt(var + EPS)
    return (y * g.astype(jnp.float32) + b.astype(jnp.float32)).astype(x.dtype)


def gmlp_group(zu, zv, ln_v_g, ln_v_b, w_s, b_s):
    B, S, _ = zu.shape
    u = jax.nn.gelu(zu)
    v = layer_norm(jax.nn.gelu(zv), ln_v_g, ln_v_b)
    vb = v.reshape(B, S // CHUNK, CHUNK, G_HEADS, G_HEAD_DIM)
    causal = jnp.tril(jnp.ones((CHUNK, CHUNK), dtype=bool))
    ws = jnp.where(causal[None], w_s, jnp.zeros_like(w_s))
    s = jnp.einsum('htp,bnphd->bnthd', ws, vb) + b_s.T[None, None, :, :, None]
    return u * s.reshape(B, S, G_WIDTH)


def causal_dwconv(x, w, b):
    S = x.shape[1]
    xp = jnp.pad(x, ((0, 0), (CONV_W - 1, 0), (0, 0)))
    y = b
    for k in range(CONV_W):
        y = y + xp[:, k:k + S, :] * w[k]
    return y


def rglru_group(xr, gr, conv_w, conv_b, w_a, b_a, w_x, b_x, lam):
    B, S, _ = xr.shape
    xc = causal_dwconv(xr, conv_w, conv_b)
    xh = xc.reshape(B, S, R_HEADS, R_HEAD_DIM)
    r = jax.nn.sigmoid(jnp.einsum('bshi,hij->bshj', xh, w_a).reshape(B, S, R_WIDTH) + b_a)
    i = jax.nn.sigmoid(jnp.einsum('bshi,hij->bshj', xh, w_x).reshape(B, S, R_WIDTH) + b_x)
    log_a = -RG_C * r.astype(jnp.float32) * jax.nn.softplus(-lam.astype(jnp.float32))
    a = jnp.exp(log_a)
    mult = jnp.sqrt(-jnp.expm1(2.0 * log_a))
    bterm = mult * (i * xc).astype(jnp.float32)

    def combine(left, right):
        a_l, b_l = left
        a_r, b_r = right
        return a_l * a_r, a_r * b_l + b_r

    _, h = lax.associative_scan(combine, (a, bterm), axis=1)
    return jax.nn.gelu(gr) * h.astype(xr.dtype)


def token_mixer(h, w_in, ln_v_g, ln_v_b, w_s, b_s, conv_w, conv_b, w_a, b_a,
                w_x, b_x, lam, g_out_gmlp, g_out_lru, w_out):
    z = h @ w_in
    zu, zv, xr, gr = jnp.split(z, [G_WIDTH, 2 * G_WIDTH, 2 * G_WIDTH + R_WIDTH], axis=-1)
    y_g = gmlp_group(zu, zv, ln_v_g, ln_v_b, w_s, b_s)
    y_r = rglru_group(xr, gr, conv_w, conv_b, w_a, b_a, w_x, b_x, lam)
    y = jnp.concatenate([rms_norm(y_g, g_out_gmlp), rms_norm(y_r, g_out_lru)], axis=-1)
    return y @ w_out


def mem_cross_attention(h, m, w_q, w_kv, w_o):
    B, S, _ = h.shape
    q = (h @ w_q).reshape(B, S, X_HEADS, X_HEAD_DIM)
    k, v = jnp.split(m @ w_kv, 2, axis=-1)
    k = k.reshape(B, N_MEM, X_HEADS, X_HEAD_DIM)
    v = v.reshape(B, N_MEM, X_HEADS, X_HEAD_DIM)
    scores = jnp.einsum('bshd,bmhd->bhsm', q, k).astype(jnp.float32) * (X_HEAD_DIM ** -0.5)
    p = jax.nn.softmax(scores, axis=-1).astype(v.dtype)
    o = jnp.einsum('bhsm,bmhd->bshd', p, v).reshape(B, S, X_HEADS * X_HEAD_DIM)
    return o @ w_o


def swiglu(h, w_gate, w_up, w_down):
    return (jax.nn.silu(h @ w_gate) * (h @ w_up)) @ w_down


def setup_inputs(seed: int = 0) -> dict:
    key = jax.random.key(seed)
    ks = iter(jax.random.split(key, 40))
    f32 = jnp.float32

    def nrm(shape, scale):
        return jax.random.normal(next(ks), shape, f32) * scale

    def gain(shape):
        return 1.0 + 0.02 * jax.random.normal(next(ks), shape, f32)

    Lr = DEPTH
    u = jax.random.uniform(next(ks), (Lr, R_WIDTH), f32, 0.9, 0.999)
    a0 = u ** (1.0 / RG_C)
    lam = jnp.log(a0) - jnp.log1p(-a0)
    return {
        "x": jax.random.normal(next(ks), (BATCH, SEQ, D_MODEL), f32),
        "mem": jax.random.normal(next(ks), (BATCH, N_MEM, D_MODEL), f32),
        "w_in": nrm((Lr, D_MODEL, IN_COLS), D_MODEL ** -0.5),
        "ln_v_g": gain((Lr, G_WIDTH)),
        "ln_v_b": nrm((Lr, G_WIDTH), 0.02),
        "w_s": nrm((Lr, G_HEADS, CHUNK, CHUNK), CHUNK ** -0.5),
        "b_s": gain((Lr, G_HEADS, CHUNK)),
        "conv_w": nrm((Lr, CONV_W, R_WIDTH), CONV_W ** -0.5),
        "conv_b": nrm((Lr, R_WIDTH), 0.02),
        "w_a": nrm((Lr, R_HEADS, R_HEAD_DIM, R_HEAD_DIM), R_HEAD_DIM ** -0.5),
        "b_a": nrm((Lr, R_WIDTH), 0.02),
        "w_x": nrm((Lr, R_HEADS, R_HEAD_DIM, R_HEAD_DIM), R_HEAD_DIM ** -0.5),
        "b_x": nrm((Lr, R_WIDTH), 0.02),
        "lam": lam,
        "g_out_gmlp": gain((Lr, G_WIDTH)),
        "g_out_lru": gain((Lr, R_WIDTH)),
        "w_out": nrm((Lr, D_MIX, D_MODEL), D_MIX ** -0.5),
        "w_q": nrm((Lr, D_MODEL, X_HEADS * X_HEAD_DIM), D_MODEL ** -0.5),
        "w_kv": nrm((Lr, D_MODEL, 2 * X_HEADS * X_HEAD_DIM), D_MODEL ** -0.5),
        "w_o": nrm((Lr, X_HEADS * X_HEAD_DIM, D_MODEL), (X_HEADS * X_HEAD_DIM) ** -0.5),
        "w_gate": nrm((Lr, D_MODEL, D_FF), D_MODEL ** -0.5),
        "w_up": nrm((Lr, D_MODEL, D_FF), D_MODEL ** -0.5),
        "w_down": nrm((Lr, D_FF, D_MODEL), D_FF ** -0.5),
        "n_pre_mix": gain((Lr, D_MODEL)),
        "n_post_mix": gain((Lr, D_MODEL)),
        "n_pre_x": gain((Lr, D_MODEL)),
        "n_mem": gain((Lr, D_MODEL)),
        "n_post_x": gain((Lr, D_MODEL)),
        "n_pre_ffn": gain((Lr, D_MODEL)),
        "n_post_ffn": gain((Lr, D_MODEL)),
    }


def reference(x, mem, w_in, ln_v_g, ln_v_b, w_s, b_s, conv_w, conv_b, w_a, b_a,
              w_x, b_x, lam, g_out_gmlp, g_out_lru, w_out, w_q, w_kv, w_o,
              w_gate, w_up, w_down, n_pre_mix, n_post_mix, n_pre_x, n_mem,
              n_post_x, n_pre_ffn, n_post_ffn):
    h = x
    for l in range(DEPTH):
        y = token_mixer(rms_norm(h, n_pre_mix[l]), w_in[l], ln_v_g[l], ln_v_b[l],
                        w_s[l], b_s[l], conv_w[l], conv_b[l], w_a[l], b_a[l],
                        w_x[l], b_x[l], lam[l], g_out_gmlp[l], g_out_lru[l], w_out[l])
        h = h + rms_norm(y, n_post_mix[l])
        y = mem_cross_attention(rms_norm(h, n_pre_x[l]), rms_norm(mem, n_mem[l]),
                                w_q[l], w_kv[l], w_o[l])
        h = h + rms_norm(y, n_post_x[l])
        y = swiglu(rms_norm(h, n_pre_ffn[l]), w_gate[l], w_up[l], w_down[l])
        h = h + rms_norm(y, n_post_ffn[l])
    return h
```

```python
from contextlib import ExitStack
import numpy as np
import concourse.bass as bass
import concourse.mybir as mybir
from concourse.bass_utils import run_bass_kernel_spmd

F32 = mybir.dt.float32
BF16 = mybir.dt.bfloat16
U8 = mybir.dt.uint8
ALU = mybir.AluOpType
AF = mybir.ActivationFunctionType

NCORES = 8
D = 1024
KC = 8
SEQ = 8192
SEG = 2048
NT = 512
NF = 256
N_PRE = 12
N_OWN = 4
DFF = 2816
FC = 22
EPS = 1e-6
NV = 96
LAG_P = 4
LAG_M = 10
FINE = 1


class Buf:
    __slots__ = ("name", "writer", "readers", "rng", "active")
    registry = []

    def __init__(self, name="", writer=None, rng=None):
        self.name = name
        self.writer = writer
        self.readers = []
        self.rng = rng
        self.active = False
        Buf.registry.append(self)


class Op:
    __slots__ = ("id", "eng", "fn", "deps", "cost", "is_dma", "tset", "tag", "start", "done", "count", "sem")

    def __init__(self, id_, eng, fn, deps, cost, is_dma, tset, tag):
        self.id, self.eng, self.fn, self.deps, self.cost = id_, eng, fn, deps, cost
        self.is_dma, self.tset, self.tag = is_dma, tset, tag
        self.start = self.done = None
        self.count = None
        self.sem = None


class Eng:
    def __init__(self, name, sem, selfsync):
        self.name = name
        self.sem = sem
        self.selfsync = selfsync
        self.order = []


class KB:
    def __init__(self, nc, stack, n_dma_sems=20):
        self.nc = nc
        mk = lambda name: stack.enter_context(nc.semaphore(name))
        self.pe = Eng("pe", mk("s_pe"), False)
        self.act = Eng("act", mk("s_act"), True)
        self.dve = Eng("dve", mk("s_dve"), True)
        self.pool = Eng("pool", mk("s_pool"), True)
        self.sp = Eng("sp", mk("s_sp"), False)
        self.engs = [self.pe, self.act, self.dve, self.pool, self.sp]
        self.dma_pools = {"sp": [mk(f"s_dsp{i}") for i in range(n_dma_sems)],
                          "pool": [mk(f"s_dpl{i}") for i in range(12)]}
        self.ops = []
        self.ranged = []
        self.makespan = None

    def _new(self, eng, fn, reads, writes, cost, is_dma, tset, tag, extra_deps):
        deps = set(extra_deps)
        for b in list(reads) + list(writes):
            if b.rng is not None and not b.active:
                b.active = True
                for o in self.ranged:
                    if o is not b and o.rng[0] < b.rng[1] and b.rng[0] < o.rng[1]:
                        if o.writer is not None:
                            deps.add(o.writer)
                        deps.update(o.readers)
                self.ranged.append(b)
        for b in reads:
            if b.writer is not None:
                deps.add(b.writer)
        for b in writes:
            if b.writer is not None:
                deps.add(b.writer)
            deps.update(b.readers)
        op = Op(len(self.ops), eng, fn, deps, cost, is_dma, tset, tag)
        self.ops.append(op)
        for b in reads:
            b.readers.append(op.id)
        for b in writes:
            b.writer = op.id
            b.readers = []
        return op.id

    def op(self, eng, fn, reads=(), writes=(), cost=0.5, tag="", tset=None, extra_deps=()):
        return self._new(eng, fn, reads, writes, cost, False, tset, tag, extra_deps)

    def dma(self, eng, fn, reads=(), writes=(), cost=10.0, tag="dma", extra_deps=()):
        return self._new(eng, fn, reads, writes, cost, True, None, tag, extra_deps)

    def barrier(self, eng):
        return self._new(eng, None, (), (), 0.0, False, None, "barrier", range(len(self.ops)))

    def join(self, eng, deps):
        return self._new(eng, None, (), (), 0.0, False, None, "join", deps)

    def schedule(self):
        ops = self.ops
        n = len(ops)
        succ = [[] for _ in range(n)]
        unmet = [0] * n
        for o in ops:
            o.deps.discard(o.id)
            unmet[o.id] = len(o.deps)
            for d in o.deps:
                succ[d].append(o.id)
        ready_t = [0.0] * n
        ready = {e.name: [] for e in self.engs}
        for o in ops:
            if unmet[o.id] == 0:
                ready[o.eng.name].append(o.id)
        tfree = {e.name: 0.0 for e in self.engs}
        last_tset = [None]
        remaining = n
        for e in self.engs:
            e.order = []
        while remaining:
            best = None
            for e in self.engs:
                rl = ready[e.name]
                if not rl:
                    continue
                tf = tfree[e.name]
                cand = None
                for oid in rl:
                    st = max(tf, ready_t[oid])
                    key = st
                    o = ops[oid]
                    if e.name == "act" and o.tset and last_tset[0] and o.tset != last_tset[0]:
                        key += 1.3
                    k = (key, oid)
                    if cand is None or k < cand[0]:
                        cand = (k, oid, st)
                if best is None or (cand[2], cand[1]) < (best[2], best[1]):
                    best = cand + (e,)
            _, oid, st, e = best
            o = ops[oid]
            ready[e.name].remove(oid)
            if e.name == "act" and o.tset:
                if last_tset[0] and o.tset != last_tset[0]:
                    st += 1.3
                last_tset[0] = o.tset
            o.start = st
            if o.is_dma:
                tfree[e.name] = st + 0.1
                o.done = st + o.cost
            else:
                tfree[e.name] = st + o.cost
                o.done = st + o.cost
            e.order.append(oid)
            remaining -= 1
            for s in succ[oid]:
                unmet[s] -= 1
                rt_ = o.done + (0.5 if ops[s].eng is not e else 0.15)
                if ready_t[s] < rt_:
                    ready_t[s] = rt_
                if unmet[s] == 0:
                    ready[ops[s].eng.name].append(s)
        self.makespan = max(o.done for o in ops)
        return self.makespan

    def emit(self, block):
        ops = self.ops
        if self.makespan is None:
            self.schedule()
        for e in self.engs:
            cnt = 0
            rr = 0
            slots = {}
            for oid in e.order:
                o = ops[oid]
                if o.fn is None:
                    continue
                if o.is_dma:
                    sems_ = self.dma_pools[e.name]
                    k = rr % len(sems_)
                    rr += 1
                    prev = slots.get(k, (0, None))
                    o.sem, o.count = sems_[k], prev[0] + 16
                    slots[k] = (o.count, oid)
                else:
                    cnt += 1
                    o.sem, o.count = e.sem, cnt
        none_need = {}

        def merge(dst, key, v, s):
            if dst.get(key, (0, None))[0] < v:
                dst[key] = (v, s)

        def need_of_none(o):
            if o.id in none_need:
                return none_need[o.id]
            need = {}
            if o.tag == "barrier":
                for od in ops[:o.id]:
                    if od.fn is not None:
                        merge(need, id(od.sem), od.count, od.sem)
            else:
                for d in o.deps:
                    od = ops[d]
                    if od.fn is None:
                        for key, (v, s) in need_of_none(od).items():
                            merge(need, key, v, s)
                    else:
                        merge(need, id(od.sem), od.count, od.sem)
            none_need[o.id] = need
            return need

        plans = {}
        for e in self.engs:
            seen = {}
            plan = []
            rr = 0
            slot_prev = {}
            for oid in e.order:
                o = ops[oid]
                need = {}
                if o.fn is None:
                    need = dict(need_of_none(o))
                else:
                    for d in o.deps:
                        od = ops[d]
                        if od.fn is None:
                            for key, (v, s) in need_of_none(od).items():
                                merge(need, key, v, s)
                            continue
                        if od.eng is e and not od.is_dma and not e.selfsync:
                            continue
                        merge(need, id(od.sem), od.count, od.sem)
                if o.is_dma:
                    sems_ = self.dma_pools[e.name]
                    k = rr % len(sems_)
                    rr += 1
                    if k in slot_prev:
                        merge(need, id(o.sem), slot_prev[k], o.sem)
                    slot_prev[k] = o.count
                waits = []
                for key, (v, s) in need.items():
                    if seen.get(key, 0) >= v:
                        continue
                    seen[key] = v
                    waits.append((s, v))
                plan.append((waits, o))
            plans[e.name] = plan

        def run(eng):
            def body(e_):
                for waits, o in plans[eng.name]:
                    for s, v in waits:
                        e_.wait_ge(s, v)
                    if o.fn is None:
                        continue
                    ins = o.fn(e_)
                    ins.then_inc(o.sem, 16 if o.is_dma else 1)
            return body
        block.tensor(run(self.pe))
        block.scalar(run(self.act))
        block.vector(run(self.dve))
        block.gpsimd(run(self.pool))
        block.sync(run(self.sp))


class Arena:
    def __init__(self, mem, size):
        self.mem = mem
        self.size = size
        self.off = 0

    def alloc(self, free_shape, dtype, name=""):
        esz = 2 if dtype == BF16 else 4
        n = 1
        for s in free_shape:
            n *= s
        nbytes = (n * esz + 63) // 64 * 64
        assert self.off + nbytes <= self.size, (name, self.off, nbytes, self.size)
        ap = self.mem[:, self.off:self.off + n * esz].bitcast(dtype)
        self.off += nbytes
        if len(free_shape) == 2:
            ap = ap.rearrange("p (a b) -> p a b", a=free_shape[0])
        return ap


def bufs(n, name):
    return [Buf(f"{name}{i}") for i in range(n)]


def build_program():
    nc = bass.Bass("TRN2", target_bir_lowering=False)

    def din(name, shape):
        return nc.dram_tensor(name, shape, F32, kind="ExternalInput").ap()
    xsT = din("xsT", [D, SEQ])
    memT = din("memT", [D, 256])
    w_in_d = din("w_in", [D, 2048])
    w_out_d = din("w_out", [D, D])
    w_q_d = din("w_q", [D, D])
    w_kv_d = din("w_kv", [D, 2048])
    w_o_d = din("w_o", [D, D])
    w_gate_d = din("w_gate", [D, DFF])
    w_up_d = din("w_up", [D, DFF])
    w_down_d = din("w_down", [DFF, D])
    vecs_d = din("vecs", [128, NV])
    lnrow_d = din("lnrow", [128, 1024])
    wsT_d = din("wsT", [128, 512])
    cmask_d = din("cmask", [128, 128])
    wab_d = din("wab", [128, 512])
    wxb_d = din("wxb", [128, 512])
    bsrow_d = din("bsrow", [1, 512])
    tmask_d = din("tmask", [128, N_PRE])
    outT = nc.dram_tensor("outT", [D, SEG], F32, kind="ExternalOutput").ap()
    h2d = nc.dram_tensor("h2d", [D, SEG], F32).ap()
    h1d = nc.dram_tensor("h1d", [D, SEG], F32).ap()

    with ExitStack() as st:
        TOTAL = 211968
        mem = st.enter_context(nc.sbuf_tensor("mem", [128, TOTAL], U8))
        psum = [st.enter_context(nc.psum_tensor(f"ps{i}", [128, 512], F32)) for i in range(8)]
        psb = bufs(8, "ps")
        kb = KB(nc, st)
        pe, act, dve, pool, sp = kb.pe, kb.act, kb.dve, kb.pool, kb.sp
        bank_rr = [0]

        def bank():
            i = bank_rr[0] % 8
            bank_rr[0] += 1
            return psum[i], psb[i]

        ar = Arena(mem, TOTAL)
        vecs = ar.alloc([NV], F32, "vecs")
        ones1024 = ar.alloc([128], BF16)
        ones1 = ar.alloc([128], BF16)
        B_vecs, B_const = Buf("vecs"), Buf("const")
        F_BASE = ar.off

        def ralloc(shape, dtype, name):
            off = ar.off
            ap = ar.alloc(shape, dtype, name)
            return ap, Buf(name, rng=(off, ar.off))
        der, B_der = ralloc([32], F32, "der")
        tmask, B_tmask = ralloc([N_PRE], F32, "tmask")
        state_off = ar.off
        state = ar.alloc([4], F32, "state")
        B_state = [Buf("state%d" % i, rng=(state_off, ar.off)) for i in range(4)]
        ones512, B_c512 = ralloc([128], BF16, "ones512")
        lng_off = ar.off
        lng = ar.alloc([512], F32)
        lnb = ar.alloc([512], F32)
        B_ln = Buf("ln", rng=(lng_off, ar.off))
        wsm, B_wsm = ralloc([4, 128], BF16, "wsm")
        wab_off = ar.off
        wab = ar.alloc([4, 128], BF16)
        wxb = ar.alloc([4, 128], BF16)
        B_wab = Buf("wab", rng=(wab_off, ar.off))
        bsrow, B_bs = ralloc([512], BF16, "bsrow")
        stt, B_stt = ralloc([40], F32, "stt")
        PB_END = ar.off

        def alias(off, free_shape, dtype):
            esz = 2 if dtype == BF16 else 4
            n = 1
            for s_ in free_shape:
                n *= s_
            ap = mem[:, off:off + n * esz].bitcast(dtype)
            if len(free_shape) == 2:
                ap = ap.rearrange("p (a b) -> p a b", a=free_shape[0])
            return ap

        class Set:
            pass

        def rbufs(off, nbytes, nb, name):
            step = nbytes // nb
            return [Buf("%s%d" % (name, i), rng=(off + i * step, off + (i + 1) * step)) for i in range(nb)]

        def allocb(shape, dtype, nb, name=""):
            off = ar.off
            ap = ar.alloc(shape, dtype, name)
            n = 1
            for s_ in shape:
                n *= s_
            return ap, rbufs(off, n * (2 if dtype == BF16 else 4), nb, name), off

        def make_set(name, N, kind, xrh=None, B_xrh=None):
            S = Set()
            S.N, S.NC, S.name = N, N // 128, name
            S.xrh, S.B_xrh = xrh, B_xrh
            S.xt, S.B_xt, _ = allocb([KC, N], F32, 8, "xt")
            S.xn, S.B_xn, xn_off = allocb([KC, N], BF16, 8, "xn")
            S.sq, S.B_sq, sq_off = allocb([KC, N], BF16, 8, "sq")
            nslot = 1 if kind == "P" else 2
            mse_, bm_, _ = allocb([N], F32, 1, "mse")
            S.mse, S.B_mse = mse_, bm_[0]
            S.rstd, S.B_rstd = [], []
            for _i in range(nslot):
                r_, br_, _ = allocb([N], F32, 1, "rstd")
                S.rstd.append(r_)
                S.B_rstd.append(br_[0])
            S.rr = 0
            if kind == "A":
                S.yg, S.B_yg, yg_off = allocb([KC, N], F32, 8, "yg")
                S.qT = alias(yg_off, [KC, N], BF16)
                S.B_qT = [S.B_yg[j // 2] for j in range(8)]
                S.rc = [alias(yg_off + KC * N * 2 + r_ * N * 4, [N], F32) for r_ in range(2)]
                S.B_rc = [[S.B_yg[4 + r_ * (N * 4 // (N * 4))]] for r_ in range(2)]
                return S
            S.hs, S.B_hs, _ = allocb([4, N], F32, 4, "hs")
            S.thi4, S.thr4 = alias(xn_off, [4, N], F32), alias(sq_off, [4, N], F32)
            S.B_thi4 = [[S.B_xn[2 * c], S.B_xn[2 * c + 1]] for c in range(4)]
            S.B_thr4 = [[S.B_sq[2 * c], S.B_sq[2 * c + 1]] for c in range(4)]
            if kind != "P":
                xcb_, S.B_xcb4, _ = allocb([4, N], BF16, 4, "xcb")
                S.xcb4 = [xcb_[:, c, :] for c in range(4)]
                S.xcb = xcb_
            if kind == "P":
                S.xb, S.B_xb, _ = allocb([KC, N], BF16, 8, "xb")
                S.tm = S.xt[:, 4:8, :]
                S.ta4 = [S.xt[:, c, :] for c in range(4)]
                S.B_ta4 = [[S.B_xt[c]] for c in range(4)]
                S.tm4 = [S.xt[:, 4 + c, :] for c in range(4)]
                S.B_tm4 = [S.B_xt[4 + c] for c in range(4)]
            else:
                tm_, S.B_tm4, _ = allocb([4, N], F32, 4, "tm")
                S.tm4 = [tm_[:, c, :] for c in range(4)]
                S.tm = tm_
                S.gu, S.B_gu, _ = allocb([4, N], F32, 4, "gu")
                S.ggr, S.B_ggr, _ = allocb([4, N], F32, 4, "ggr")
                S.gv, S.B_gv, gv_off = allocb([S.NC, 512], F32, S.NC, "gv")
                S.vtm, S.B_vtm, _ = allocb([S.NC, 512], BF16, S.NC, "vtm")
                gvv = alias(gv_off, [4, N], F32)
                S.ta4 = [gvv[:, i, :] for i in range(4)]
                S.B_ta4 = [[S.B_gv[i * S.NC // 4]] for i in range(4)]
            return S

        NM = 256
        XW = 2 * (NM + 4)
        xrh_off = [ar.off, ar.off + 4 * XW * 4]
        xrhP = [ar.alloc([4, XW], F32, "xrh0"), ar.alloc([4, XW], F32, "xrh1")]

        def xrh_bufs(i, c0, c1, name):
            return [Buf("%s%d" % (name, c), rng=(xrh_off[i] + (c * XW + c0) * 4, xrh_off[i] + (c * XW + c1) * 4))
                    for c in range(4)]
        R0 = ar.off
        w_in, B_win1, win_off = allocb([KC, 2048], BF16, 1, "w_in")
        win_rng = B_win1[0].rng
        B_win = [Buf("w_in%d" % j, rng=win_rng) for j in range(4)]
        W_BASE = ar.off
        ar.off = R0
        S0 = make_set("p0", NT, "P", xrhP[0][:, :, 0:NT + 4], xrh_bufs(0, 0, NT + 4, "xrhp0"))
        S2 = make_set("p2", NT, "P")
        S1 = make_set("p1", NT, "P", xrhP[1][:, :, 0:NT + 4], xrh_bufs(1, 0, NT + 4, "xrhp1"))
        x2_off = ar.off
        xrh2 = ar.alloc([4, XW], F32, "xrh2")
        S2.xrh = xrh2[:, :, 0:NT + 4]
        S2.B_xrh = [Buf("xrhp2%d" % c, rng=(x2_off + c * XW * 4, x2_off + (c * XW + NT + 4) * 4)) for c in range(4)]
        w_xr, bwx_, _ = allocb([KC, 512], BF16, 1, "w_xr")
        B_wxr = bwx_[0]
        wab32, b32a_, _ = allocb([4, 128], F32, 1, "wab32")
        wxb32, b32x_, _ = allocb([4, 128], F32, 1, "wxb32")
        B_w32 = [b32a_[0], b32x_[0]]
        PS = [S0, S2, S1]
        P_END = ar.off
        ar.off = W_BASE
        set_off = []
        MS = []
        xv = [(1, 0, NM + 4), (0, 0, NM + 4), (0, NM + 4, XW)]
        for i in range(3):
            set_off.append(ar.off)
            ii, c0, c1 = xv[i]
            MS.append(make_set("m%d" % i, NM, "M", xrhP[ii][:, :, c0:c1], xrh_bufs(ii, c0, c1, "xrhm%d" % i)))
        set_off.append(ar.off)
        w_out, bwo_, _ = allocb([KC, D], BF16, 1, "w_out")
        B_wout = bwo_[0]
        assert ar.off <= TOTAL
        B_h1d = bufs(SEG // NM, "h1d")
        B_h2d = bufs(SEG // NM, "h2d")
        ar.off = win_off
        wkvK, bk_, _ = allocb([KC, D], BF16, 1, "wkvK")
        wkvV, bv_, _ = allocb([KC, D], BF16, 1, "wkvV")
        B_wkvK, B_wkvV = bk_[0], bv_[0]
        ar.off = xrh_off[0]
        mt, B_mt, _ = allocb([KC, 256], F32, 8, "mt")
        mn, B_mn, _ = allocb([KC, 256], BF16, 8, "mn")
        msq, B_msq, _ = allocb([KC, 256], BF16, 8, "msq")
        assert ar.off <= R0
        ar.off = set_off[2]
        w_q, bq_, _ = allocb([KC, D], BF16, 1, "w_q")
        w_o, bo_, _ = allocb([KC, D], BF16, 1, "w_o")
        kvt, bkv_, _ = allocb([4096], BF16, 1, "kvt")
        B_wq, B_wo, B_kT = bq_[0], bo_[0], bkv_[0]
        B_vv = B_kT
        kT = kvt[:, 0:2048].rearrange("p (a b) -> p a b", a=8)
        vv = kvt[:, 2048:4096].rearrange("p (a b) -> p a b", a=2)
        KS = Set()
        KS.N = 256
        kmse_, bkm_, _ = allocb([256], F32, 1, "kmse")
        krs_, bkr_, _ = allocb([256], F32, 1, "krstd")
        KS.mse, KS.B_mse, KS.rstd, KS.B_rstd, KS.rr = kmse_, bkm_[0], [krs_], [bkr_[0]], 0
        assert ar.off <= set_off[3], (ar.off, set_off)
        ar.off = set_off[0]
        AS = [make_set("a%d" % i, NM, "A") for i in range(3)]
        assert ar.off <= set_off[2], (ar.off, set_off)

        def V(col):
            return vecs[:, col:col + 1]

        def mm_group(out_ap, pairs, reads, writes, f32=False):
            n = len(pairs) * (4 if f32 else 1)

            npair = len(pairs)

            def fn(e):
                ins = None
                for i, (l, r) in enumerate(pairs):
                    ins = e.matmul(out_ap, lhsT=l, rhs=r, start=(i == 0), stop=(i == npair - 1))
                return ins
            nmov = pairs[0][1].free_size()
            per = {512: 0.285, 256: 0.118}.get(nmov, 0.11 + nmov * 0.0004)
            kb.op(pe, fn, reads, writes, cost=n * per, tag='mm%dx%d' % (n, nmov))

        def A(out, in_, func, reads, writes, **kw):
            nm = str(func).split('.')[-1]
            tset = {"Gelu_apprx_tanh": "g", "Tanh": "g", "Exp": "e", "Ln": "e", "Silu": "s"}.get(nm)
            kb.op(act, lambda e: e.activation(out=out, in_=in_, func=func, **kw), reads, writes,
                  cost=0.2 + out.free_size() * 0.00085, tag=nm, tset=tset)

        def dcost(out, f=1.0):
            return 0.08 + out.free_size() * 0.00105 * f

        def TT(eng, out, in0, in1, op, reads, writes):
            kb.op(eng, lambda e: e.tensor_tensor(out=out, in0=in0, in1=in1, op=op), reads, writes, cost=dcost(out), tag='tt')

        def TS(eng, out, in0, s1, s2, op0, op1, reads, writes):
            if op1 is None:
                kb.op(eng, lambda e: e.tensor_scalar(out=out, in0=in0, scalar1=s1, scalar2=None, op0=op0), reads, writes,
                      cost=dcost(out, 0.7), tag='ts')
            else:
                kb.op(eng, lambda e: e.tensor_scalar(out=out, in0=in0, scalar1=s1, scalar2=s2, op0=op0, op1=op1),
                      reads, writes, cost=dcost(out, 0.7), tag='ts')

        def STT(out, in0, scalar, in1, op0, op1, reads, writes):
            kb.op(dve, lambda e: e.scalar_tensor_tensor(out=out, in0=in0, scalar=scalar, in1=in1, op0=op0, op1=op1),
                  reads, writes, cost=dcost(out, 1.25), tag='stt')

        def stats_rstd(S, sq_aps, sq_bufs, ones_t, cb=None):
            n = S.N
            i = S.rr % len(S.rstd)
            S.rr += 1
            ps, pb = bank()
            mm_group(ps[:, 0:n], [(ones_t[:, :], s) for s in sq_aps], list(sq_bufs) + [cb or B_const], [pb])
            A(S.mse, ps[:, 0:n], AF.Ln, [pb], [S.B_mse], bias=EPS)
            A(S.rstd[i], S.mse, AF.Exp, [S.B_mse], [S.B_rstd[i]], scale=-0.5)
            return S.rstd[i], S.B_rstd[i]

        def rms_stats(S, src, src_b, C, ones_t, sqt, sq_b, cb=None, src_all=None, sq_all=None):
            if src_all is not None:
                A(sq_all, src_all, AF.Square, list(src_b[0:C]), list(sq_b[0:C]))
                yield
            else:
                for c in range(C):
                    A(sqt(c), src(c), AF.Square, [src_b[c]], [sq_b[c]])
                    if c % FINE == FINE - 1:
                        yield
            S.last_r = stats_rstd(S, [sqt(c) for c in range(C)], [sq_b[c] for c in range(C)], ones_t, cb)
            yield

        def rms_apply(src, src_b, C, gcol, r, rb, dst, dst_b):
            for c in range(C):
                STT(dst(c), src(c), V(gcol + c), r, ALU.mult, ALU.mult, [src_b[c], rb, B_vecs], [dst_b[c]])
                if c % FINE == FINE - 1:
                    yield

        def proj_evac(S, w_sb, w_b, rhs, rhs_b, gcol, yg, yg_b, sqt, sq_b, nk=KC):
            n = S.N
            for oc in range(KC):
                ps, pb = bank()
                mm_group(ps[:, 0:n], [(w_sb[:, k, oc * 128:(oc + 1) * 128], rhs(k)) for k in range(nk)],
                         list(w_b) + [rhs_b[k] for k in range(nk)], [pb])
                A(yg(oc), ps[:, 0:n], AF.Identity, [pb, B_vecs], [yg_b[oc]], scale=V(gcol + oc))
                A(sqt(oc), ps[:, 0:n], AF.Square, [pb], [sq_b[oc]])
                if oc % FINE == FINE - 1:
                    yield

        def postnorm_res(S, res, res_b, yg, yg_b, sqt, sq_b):
            r, rb = stats_rstd(S, [sqt(c) for c in range(KC)], sq_b, ones1024)
            yield
            for oc in range(KC):
                TT(dve, yg(oc), yg(oc), r, ALU.mult, [yg_b[oc], rb], [yg_b[oc]])
                TT(dve, res(oc), res(oc), yg(oc), ALU.add, [res_b[oc], yg_b[oc]], [res_b[oc]])
                if oc % FINE == FINE - 1:
                    yield

        def wload(dst, src, wr, extra=()):
            kb.dma(pool, lambda e: e.dma_start(out=dst, in_=src.rearrange("(k p) c -> p k c", p=128)), writes=wr,
                   cost=3.0 + dst.free_size() * 128 * 4 / 250e3, extra_deps=extra)

        kb.dma(sp, lambda e: e.dma_start(out=vecs, in_=vecs_d), writes=[B_vecs])
        kb.dma(sp, lambda e: e.dma_start(out=tmask, in_=tmask_d), writes=[B_tmask])
        kb.op(pool, lambda e: e.memset(ones1024, 1.0 / 1024.0), writes=[B_const])
        kb.op(pool, lambda e: e.memset(ones512, 1.0 / 512.0), writes=[B_c512])
        kb.op(pool, lambda e: e.memset(ones1, 1.0), writes=[B_const])
        kb.op(pool, lambda e: e.memset(state, 0.0), writes=B_state)
        kb.op(pool, lambda e: e.memset(xrhP[0], 0.0), writes=S0.B_xrh)
        kb.op(pool, lambda e: e.memset(xrhP[1], 0.0), writes=S1.B_xrh)
        kb.op(pool, lambda e: e.memset(xrh2, 0.0), writes=S2.B_xrh)
        kb.dma(sp, lambda e: e.dma_start(out=S0.xt, in_=w_in_d[:, 1024:1536].rearrange("(k p) c -> p k c", p=128)),
               writes=S0.B_xt, cost=10.0)
        for k_ in range(KC):
            TS(dve, w_xr[:, k_, :], S0.xt[:, k_, :], V(k_), None, ALU.mult, None, [S0.B_xt[k_], B_vecs], [B_wxr])
        kb.dma(sp, lambda e: e.dma_start(out=wab32.rearrange("p a b -> p (a b)"), in_=wab_d), writes=[B_w32[0]], cost=3.0)
        kb.dma(sp, lambda e: e.dma_start(out=wxb32.rearrange("p a b -> p (a b)"), in_=wxb_d), writes=[B_w32[1]], cost=3.0)
        TS(dve, der[:, 0:4], vecs[:, 84:88], 0.5, None, ALU.mult, None, [B_vecs], [B_der])
        TS(dve, der[:, 4:8], vecs[:, 88:92], 0.5, None, ALU.mult, None, [B_vecs], [B_der])
        A(der[:, 16:20], vecs[:, 92:96], AF.Exp, [B_vecs], [B_der], scale=-1.0)
        A(der[:, 20:24], der[:, 16:20], AF.Ln, [B_der], [B_der], bias=1.0)
        TS(dve, der[:, 12:16], der[:, 20:24], -4.0, None, ALU.mult, None, [B_der], [B_der])
        TS(dve, der[:, 8:12], der[:, 20:24], -8.0, None, ALU.mult, None, [B_der], [B_der])

        def load_x(tok0, S, extra=()):
            return kb.dma(sp, lambda e: e.dma_start(
                out=S.xt, in_=xsT[:, tok0:tok0 + S.N].rearrange("(k p) n -> p k n", p=128)), writes=S.B_xt,
                cost=3.0 + S.N * 0.014, extra_deps=extra)

        CL = 0.9999999
        LNH = -0.6931471805599453

        def xr_proj(S, Snext, compact=False, post=None):
            N = S.N
            for c in range(4):
                ps, pb = bank()
                if compact:
                    prs = [(w_xr[:, k, c * 128:(c + 1) * 128], S.xb[:, k, :]) for k in range(KC)]
                    wb_, xb_ = B_wxr, S.B_xb
                else:
                    prs = [(w_in[:, k, 1024 + c * 128:1024 + (c + 1) * 128], S.xn[:, k, :]) for k in range(KC)]
                    wb_, xb_ = B_win[2], S.B_xn
                mm_group(ps[:, 0:N], prs, [wb_] + xb_, [pb])
                if post is not None:
                    TT(dve, S.xrh[:, c, 3:3 + N], ps[:, 0:N], post[0], ALU.mult, [pb, post[1]], [S.B_xrh[c]])
                else:
                    A(S.xrh[:, c, 3:3 + N], ps[:, 0:N], AF.Copy, [pb], [S.B_xrh[c]])
                if Snext is not S:
                    A(Snext.xrh[:, c, 0:3], S.xrh[:, c, N:N + 3], AF.Copy, [S.B_xrh[c]], [Snext.B_xrh[c]])
                if c % FINE == FINE - 1:
                    yield

        def lru_conv(S, same_set_halo):
            N = S.N
            for c in range(4):
                xc, xr = S.hs[:, c, :], S.xrh[:, c, :]
                bxr, bxc = S.B_xrh[c], S.B_hs[c]
                TS(dve, xc, xr[:, 3:3 + N], V(64 + 12 + c), V(80 + c), ALU.mult, ALU.add, [bxr, B_vecs], [bxc])
                for k in range(3):
                    STT(xc, xr[:, k:k + N], V(64 + 4 * k + c), xc, ALU.mult, ALU.add, [bxr, bxc, B_vecs], [bxc])
                if c % FINE == FINE - 1:
                    yield
            if same_set_halo:
                A(S.xrh[:, :, 0:3], S.xrh[:, :, N:N + 3], AF.Copy, list(S.B_xrh), list(S.B_xrh))
            if S.N != NT:
                A(S.xcb, S.hs, AF.Copy, list(S.B_hs), list(S.B_xcb4))
            yield

        def lru_gates(S):
            N = S.N
            pss = []
            for c in range(4):
                psa, pba = bank()
                psx, pbx = bank()
                if N == NT:
                    mm_group(psa[:, 0:N], [(wab32[:, c, :], S.hs[:, c, :])], [B_w32[0], S.B_hs[c]], [pba], f32=True)
                    mm_group(psx[:, 0:N], [(wxb32[:, c, :], S.hs[:, c, :])], [B_w32[1], S.B_hs[c]], [pbx], f32=True)
                else:
                    mm_group(psa[:, 0:N], [(wab[:, c, :], S.xcb4[c])], [B_wab, S.B_xcb4[c]], [pba])
                    mm_group(psx[:, 0:N], [(wxb[:, c, :], S.xcb4[c])], [B_wab, S.B_xcb4[c]], [pbx])
                A(S.thr4[:, c, :], psa[:, 0:N], AF.Tanh, [pba, B_der], S.B_thr4[c], scale=0.5, bias=der[:, c:c + 1])
                A(S.thi4[:, c, :], psx[:, 0:N], AF.Tanh, [pbx, B_der], S.B_thi4[c], scale=0.5, bias=der[:, 4 + c:5 + c])
                if c % FINE == FINE - 1:
                    yield

        def lru_b1(S):
            for c in range(4):
                kq = der[:, 12 + c:13 + c]
                kq2 = der[:, 8 + c:9 + c]
                A(S.ta4[c], S.thr4[:, c, :], AF.Exp, S.B_thr4[c] + [B_der], S.B_ta4[c], scale=kq, bias=kq)
                A(S.tm4[c], S.thr4[:, c, :], AF.Exp, S.B_thr4[c] + [B_der], [S.B_tm4[c]], scale=kq2, bias=kq2)
            yield
            TS(dve, S.tm, S.tm, CL, None, ALU.min, None, list(S.B_tm4), list(S.B_tm4))
            yield

        def lru_b2(S):
            B_thi_all = [b_ for l_ in S.B_thi4 for b_ in l_]
            A(S.tm, S.tm, AF.Ln, list(S.B_tm4), list(S.B_tm4), scale=-1.0, bias=1.0)
            A(S.tm, S.tm, AF.Exp, list(S.B_tm4), list(S.B_tm4), scale=0.5, bias=LNH)
            yield
            STT(S.thi4, S.thi4, 1.0, S.hs, ALU.add, ALU.mult, B_thi_all + list(S.B_hs), B_thi_all)
            TT(dve, S.thi4, S.thi4, S.tm, ALU.mult, B_thi_all + list(S.B_tm4), B_thi_all)
            yield

        def lru_scan(S, tmask_col, own):
            N = S.N
            for c in range(4):
                kb.op(dve, lambda e, c=c: e.tensor_tensor_scan(out=S.hs[:, c, :], data0=S.ta4[c], data1=S.thi4[:, c, :],
                                                               initial=state[:, c:c + 1], op0=ALU.mult, op1=ALU.add),
                      S.B_ta4[c] + [B_state[c]] + S.B_thi4[c], [S.B_hs[c]], cost=0.1 + 0.0023 * N, tag="scan")
                if own:
                    kb.op(dve, lambda e, c=c: e.tensor_copy(out=state[:, c:c + 1], in_=S.hs[:, c, N - 1:N]),
                          [S.B_hs[c]], [B_state[c]], cost=0.1)
                    TT(dve, S.hs[:, c, :], S.hs[:, c, :], S.ggr[:, c, :], ALU.mult, [S.B_hs[c], S.B_ggr[c]], [S.B_hs[c]])
                else:
                    TT(dve, state[:, c:c + 1], S.hs[:, c, N - 1:N], tmask[:, tmask_col:tmask_col + 1], ALU.mult,
                       [S.B_hs[c], B_tmask], [B_state[c]])
                if c % FINE == FINE - 1:
                    yield

        def premix(S):
            yield from rms_stats(S, lambda c: S.xt[:, c, :], S.B_xt, KC, ones1024, lambda c: S.sq[:, c, :], S.B_sq,
                                 src_all=S.xt, sq_all=S.sq)
            r, rb = S.last_r
            yield from rms_apply(lambda c: S.xt[:, c, :], S.B_xt, KC, 0, r, rb, lambda c: S.xn[:, c, :], S.B_xn)

        def run_pipelined(make_gen, n_tiles, lag, hook=None, k=2):
            gens, count, nxt = [], {}, 0
            while nxt < n_tiles or gens:
                if nxt < n_tiles and (len(gens) == 0 or (len(gens) < k and count[gens[-1][0]] >= lag)):
                    gens.append((nxt, make_gen(nxt)))
                    count[nxt] = 0
                    nxt += 1
                    if hook is not None:
                        hook(nxt)
                for tt, g in list(gens):
                    try:
                        next(g)
                        count[tt] += 1
                    except StopIteration:
                        gens.remove((tt, g))

        p_loads = []

        def p_tile(t):
            S = PS[t % 3]
            last = (t == N_PRE - 1)
            Sn = S if last else PS[(t + 1) % 3]
            ex_ = list(p_loads) if t < 3 else []
            l1 = load_x(t * NT, S, ex_)
            l2 = kb.dma(pool, lambda e: e.dma_start(
                out=S.xb, in_=xsT[:, t * NT:(t + 1) * NT].rearrange("(k p) n -> p k n", p=128)), writes=S.B_xb,
                cost=12.0, extra_deps=ex_)
            p_loads[:] = [l1, l2]
            yield
            yield from rms_stats(S, lambda c: S.xt[:, c, :], S.B_xt, KC, ones1024, lambda c: S.sq[:, c, :], S.B_sq,
                                 src_all=S.xt, sq_all=S.sq)
            yield from xr_proj(S, Sn, compact=True, post=S.last_r)
            yield from lru_conv(S, last)
            yield from lru_gates(S)
            yield from lru_b1(S)
            yield from lru_b2(S)
            yield from lru_scan(S, t, False)

        def p_hook(nxt):
            if nxt == 2:
                kb.dma(sp, lambda e: e.dma_start(out=lng, in_=lnrow_d[:, 0:512]), writes=[B_ln])
                kb.dma(sp, lambda e: e.dma_start(out=lnb, in_=lnrow_d[:, 512:1024]), writes=[B_ln])
        run_pipelined(p_tile, N_PRE, LAG_P, p_hook, k=3)
        for j_ in (0, 3, 1, 2):
            wload(w_in[:, :, j_ * 512:(j_ + 1) * 512], w_in_d[:, j_ * 512:(j_ + 1) * 512], [B_win[j_]])
        wload(w_out, w_out_d, [B_wout])

        MB = MS[2]
        kb.dma(sp, lambda e: e.dma_start(out=MB.gv[:, 0, :], in_=wsT_d), writes=[MB.B_gv[0]])
        kb.dma(sp, lambda e: e.dma_start(out=MB.gv[:, 1, 0:128], in_=cmask_d), writes=[MB.B_gv[1]])
        for h in range(4):
            TT(dve, wsm[:, h, :], MB.gv[:, 0, h * 128:(h + 1) * 128], MB.gv[:, 1, 0:128], ALU.mult,
               [MB.B_gv[0], MB.B_gv[1]], [B_wsm])
        tC = MB.ggr[:, 0:2, :].rearrange("p a b -> p (a b)")
        tD = MB.ggr[:, 2:4, :].rearrange("p a b -> p (a b)")
        kb.dma(sp, lambda e: e.dma_start(out=tC, in_=wab_d), writes=MB.B_ggr[0:2])
        kb.dma(sp, lambda e: e.dma_start(out=tD, in_=wxb_d), writes=MB.B_ggr[2:4])
        kb.op(dve, lambda e: e.tensor_copy(out=wab.rearrange("p a b -> p (a b)"), in_=tC), MB.B_ggr[0:2], [B_wab])
        kb.op(dve, lambda e: e.tensor_copy(out=wxb.rearrange("p a b -> p (a b)"), in_=tD), MB.B_ggr[2:4], [B_wab])
        kb.dma(sp, lambda e: e.dma_start(out=MB.gu[0:1, 0:2, :].rearrange("p a b -> p (a b)"), in_=bsrow_d),
               writes=MB.B_gu[0:2])
        kb.op(dve, lambda e: e.tensor_copy(out=bsrow[0:1, :], in_=MB.gu[0:1, 0:2, :].rearrange("p a b -> p (a b)")),
              MB.B_gu[0:2], [B_bs])

        mv = stt[:, 24:32]
        rs4 = stt[:, 32:36]
        OWN0 = N_PRE * NT
        NMT = SEG // NM

        def mix_tile(i):
            S = MS[i % 3]
            Sn = MS[(i + 1) % 3]
            N, NC = S.N, S.NC
            xt, xn, sq, hs, gu, ggr = S.xt, S.xn, S.sq, S.hs, S.gu, S.ggr
            B_xt, B_xn, B_sq, B_hs, B_gu, B_ggr = S.B_xt, S.B_xn, S.B_sq, S.B_hs, S.B_gu, S.B_ggr
            yg = lambda oc: (gu[:, oc, :] if oc < 4 else hs[:, oc - 4, :])
            B_yg = B_gu + B_hs
            sqt = lambda c: sq[:, c, :]
            load_x(OWN0 + i * N, S)
            yield
            yield from premix(S)
            for c in range(4):
                ps, pb = bank()
                mm_group(ps[:, 0:N], [(w_in[:, k, c * 128:(c + 1) * 128], xn[:, k, :]) for k in range(KC)],
                         [B_win[0]] + B_xn, [pb])
                A(gu[:, c, :], ps[:, 0:N], AF.Gelu_apprx_tanh, [pb], [B_gu[c]])
                if c % FINE == FINE - 1:
                    yield
            for c in range(4):
                ps, pb = bank()
                mm_group(ps[:, 0:N], [(w_in[:, k, 1536 + c * 128:1536 + (c + 1) * 128], xn[:, k, :]) for k in range(KC)],
                         [B_win[3]] + B_xn, [pb])
                A(ggr[:, c, :], ps[:, 0:N], AF.Gelu_apprx_tanh, [pb], [B_ggr[c]])
                if c % FINE == FINE - 1:
                    yield
            for n in range(NC):
                ps, pb = bank()
                mm_group(ps[:, :], [(xn[:, k, n * 128:(n + 1) * 128], w_in[:, k, 512:1024]) for k in range(KC)],
                         [B_win[1]] + B_xn, [pb])
                A(S.gv[:, n, :], ps[:, :], AF.Gelu_apprx_tanh, [pb], [S.B_gv[n]])
                yield
            yield from xr_proj(S, Sn)
            yield from lru_conv(S, False)
            yield from lru_gates(S)
            for n in range(NC):
                kb.op(dve, lambda e, n=n: e.bn_stats(out=stt[:, 6 * n:6 * n + 6], in_=S.gv[:, n, :]), [S.B_gv[n]], [B_stt])
                kb.op(dve, lambda e, n=n: e.bn_aggr(out=mv[:, 2 * n:2 * n + 2], in_=stt[:, 6 * n:6 * n + 6]),
                      [B_stt], [B_stt])
            var4 = mv[:, 0:2 * NC].rearrange("p (n two) -> p n two", two=2)[:, :, 1]
            A(rs4[:, 0:NC], var4, AF.Ln, [B_stt], [B_stt], bias=EPS)
            A(rs4[:, 0:NC], rs4[:, 0:NC], AF.Exp, [B_stt], [B_stt], scale=-0.5)
            for n in range(NC):
                gv = S.gv[:, n, :]
                TS(dve, gv, gv, mv[:, 2 * n:2 * n + 1], rs4[:, n:n + 1], ALU.subtract, ALU.mult,
                   [S.B_gv[n], B_stt], [S.B_gv[n]])
                TT(dve, gv, gv, lng, ALU.mult, [S.B_gv[n], B_ln], [S.B_gv[n]])
                TT(dve, S.vtm[:, n, :], gv, lnb, ALU.add, [S.B_gv[n], B_ln], [S.B_vtm[n]])
                yield
            for h in range(4):
                ps, pb = bank()

                def sp_fn(e, ps=ps, h=h):
                    ins = None
                    for n in range(NC):
                        e.matmul(ps[:, n * 128:(n + 1) * 128], lhsT=S.vtm[:, n, h * 128:(h + 1) * 128], rhs=wsm[:, h, :],
                                 start=True, stop=False)
                        ins = e.matmul(ps[:, n * 128:(n + 1) * 128], lhsT=ones1[0:1, :],
                                       rhs=bsrow[0:1, h * 128:(h + 1) * 128], start=False, stop=True)
                    return ins
                kb.op(pe, sp_fn, S.B_vtm + [B_wsm, B_bs, B_const], [pb], cost=0.45 * NC, tag='spatial')
                TT(dve, gu[:, h, :], gu[:, h, :], ps[:, 0:N], ALU.mult, [B_gu[h], pb], [B_gu[h]])
                if h % FINE == FINE - 1:
                    yield
            yield from lru_b1(S)
            yield from lru_b2(S)
            yield from lru_scan(S, 0, True)
            yield from rms_stats(S, lambda c: gu[:, c, :], B_gu, 4, ones512, sqt, B_sq[0:4], cb=B_c512,
                                 src_all=gu, sq_all=sq[:, 0:4, :])
            r1, rb1 = S.last_r
            yield from rms_stats(S, lambda c: hs[:, c, :], B_hs, 4, ones512, lambda c: sq[:, 4 + c, :], B_sq[4:8],
                                 cb=B_c512, src_all=hs, sq_all=sq[:, 4:8, :])
            r2, rb2 = S.last_r
            yield from rms_apply(lambda c: gu[:, c, :], B_gu, 4, 56, r1, rb1, lambda c: xn[:, c, :], B_xn[0:4])
            yield from rms_apply(lambda c: hs[:, c, :], B_hs, 4, 60, r2, rb2, lambda c: xn[:, 4 + c, :], B_xn[4:8])
            yield from proj_evac(S, w_out, [B_wout], lambda k: xn[:, k, :], B_xn, 8, yg, B_yg, sqt, B_sq)
            yield from postnorm_res(S, lambda oc: xt[:, oc, :], B_xt, yg, B_yg, sqt, B_sq)
            kb.dma(sp, lambda e: e.dma_start(
                out=h1d[:, i * N:(i + 1) * N].rearrange("(k p) n -> p k n", p=128), in_=xt), reads=B_xt,
                writes=[B_h1d[i]], cost=8.0)
            yield

        run_pipelined(mix_tile, NMT, LAG_M, k=3)

        wload(w_q, w_q_d, [B_wq])
        wload(wkvK, w_kv_d[:, 0:1024], [B_wkvK])
        wload(wkvV, w_kv_d[:, 1024:2048], [B_wkvV])
        wload(w_o, w_o_d, [B_wo])
        kb.dma(sp, lambda e: e.dma_start(out=mt, in_=memT.rearrange("(k p) n -> p k n", p=128)), writes=B_mt)
        for _ in rms_stats(KS, lambda c: mt[:, c, :], B_mt, KC, ones1024, lambda c: msq[:, c, :], B_msq):
            pass
        r, rb = KS.last_r
        for _ in rms_apply(lambda c: mt[:, c, :], B_mt, KC, 48, r, rb, lambda c: mn[:, c, :], B_mn):
            pass
        for dc in range(8):
            ps, pb = bank()
            mm_group(ps[:, 0:256], [(wkvK[:, k, dc * 128:(dc + 1) * 128], mn[:, k, :]) for k in range(KC)],
                     [B_wkvK] + B_mn, [pb])
            A(kT[:, dc, :], ps[:, 0:256], AF.Copy, [pb], [B_kT])
        for mc in range(2):
            for hf in range(2):
                ps, pb = bank()
                mm_group(ps[:, :], [(mn[:, k, mc * 128:(mc + 1) * 128], wkvV[:, k, hf * 512:(hf + 1) * 512])
                                    for k in range(KC)], [B_wkvV] + B_mn, [pb])
                A(vv[:, mc, hf * 512:(hf + 1) * 512], ps[:, :], AF.Copy, [pb], [B_vv])

        def att_tile(i):
            S = AS[i % 3]
            N = S.N
            xt, xn, sq = S.xt, S.xn, S.sq
            B_xt, B_xn, B_sq = S.B_xt, S.B_xn, S.B_sq
            yg = lambda oc: S.yg[:, oc, :]
            B_yg = S.B_yg
            sqt = lambda c: sq[:, c, :]
            kb.dma(sp, lambda e: e.dma_start(
                out=xt, in_=h1d[:, i * N:(i + 1) * N].rearrange("(k p) n -> p k n", p=128)), reads=[B_h1d[i]],
                writes=B_xt, cost=8.0)
            yield
            yield from rms_stats(S, lambda c: xt[:, c, :], B_xt, KC, ones1024, sqt, B_sq, src_all=xt, sq_all=sq)
            r, rb = S.last_r
            yield from rms_apply(lambda c: xt[:, c, :], B_xt, KC, 16, r, rb, lambda c: xn[:, c, :], B_xn)
            for oc in range(KC):
                ps, pb = bank()
                mm_group(ps[:, 0:N], [(w_q[:, k, oc * 128:(oc + 1) * 128], xn[:, k, :]) for k in range(KC)],
                         [B_wq] + B_xn, [pb])
                A(S.qT[:, oc, :], ps[:, 0:N], AF.Copy, [pb], [S.B_qT[oc]])
                if oc % FINE == FINE - 1:
                    yield
            for h in range(4):
                for mc in range(2):
                    ps, pb = bank()
                    mm_group(ps[:, 0:N], [(kT[:, 2 * h + dd, mc * 128:(mc + 1) * 128], S.qT[:, 2 * h + dd, :])
                                          for dd in range(2)], [B_kT, S.B_qT[2 * h], S.B_qT[2 * h + 1]], [pb])
                    A(sq[:, 2 * h + mc, :], ps[:, 0:N], AF.Exp, [pb], [B_sq[2 * h + mc]], scale=1.0 / 16.0)
                ps, pb = bank()
                mm_group(ps[:, 0:N], [(ones1[:, :], sq[:, 2 * h + mc, :]) for mc in range(2)],
                         [B_const, B_sq[2 * h], B_sq[2 * h + 1]], [pb])
                ri = h % 2
                A(S.rc[ri], ps[:, 0:N], AF.Ln, [pb], S.B_rc[ri])
                A(S.rc[ri], S.rc[ri], AF.Exp, S.B_rc[ri], S.B_rc[ri], scale=-1.0)
                for dd in range(2):
                    ps, pb = bank()
                    mm_group(ps[:, 0:N], [(vv[:, mc, h * 256 + dd * 128:h * 256 + (dd + 1) * 128], sq[:, 2 * h + mc, :])
                                          for mc in range(2)], [B_vv, B_sq[2 * h], B_sq[2 * h + 1]], [pb])
                    TT(dve, xn[:, 2 * h + dd, :], ps[:, 0:N], S.rc[ri], ALU.mult, [pb] + S.B_rc[ri], [B_xn[2 * h + dd]])
                yield
            yield from proj_evac(S, w_o, [B_wo], lambda k: xn[:, k, :], B_xn, 24, yg, B_yg, sqt, B_sq)
            yield from postnorm_res(S, lambda oc: xt[:, oc, :], B_xt, yg, B_yg, sqt, B_sq)
            kb.dma(sp, lambda e: e.dma_start(
                out=h2d[:, i * N:(i + 1) * N].rearrange("(k p) n -> p k n", p=128), in_=xt), reads=B_xt,
                writes=[B_h2d[i]], cost=8.0)
            yield

        run_pipelined(att_tile, NMT, 8, k=3)

        ar.off = F_BASE
        half = DFF // 2
        FH = FC // 2
        wgA, bga_, _ = allocb([KC, half], BF16, 1, "wgA")
        wuA, bua_, _ = allocb([KC, half], BF16, 1, "wuA")
        wgB, bgb_, _ = allocb([KC, half], BF16, 1, "wgB")
        wuB, bub_, _ = allocb([KC, half], BF16, 1, "wuB")
        wd, bwd_, _ = allocb([FC, D], BF16, 2, "wd")
        wgH, wuH = [wgA, wgB], [wuA, wuB]
        B_wg, B_wu, B_wd = [bga_[0], bgb_[0]], [bua_[0], bub_[0]], bwd_
        h2t, B_h2t, fxn, B_fxn = [], [], [], []
        for _i in range(2):
            t_, b_, _ = allocb([KC, NF], F32, 8, "h2t")
            h2t.append(t_)
            B_h2t.append(b_)
        for _i in range(2):
            t_, b_, _ = allocb([KC, NF], BF16, 8, "fxn")
            fxn.append(t_)
            B_fxn.append(b_)
        fsqp, B_fsqp, _ = allocb([KC, NF], BF16, 8, "fsqp")
        fsq, B_fsq, _ = allocb([KC, NF], BF16, 8, "fsq")
        fact, B_fact, _ = allocb([FC, NF], BF16, FC, "fact")
        fyg, B_fyg, _ = allocb([KC, NF], F32, 8, "fyg")
        fsg, B_fsg = [], []
        for _i in range(2):
            t_, b_, _ = allocb([NF], F32, 1, "fsg")
            fsg.append(t_)
            B_fsg.append(b_[0])
        FS = Set()
        FS.N = NF
        t_, b_, _ = allocb([NF], F32, 1, "fmse")
        FS.mse, FS.B_mse = t_, b_[0]
        FS.rstd, FS.B_rstd = [], []
        for _i in range(2):
            t_, b_, _ = allocb([NF], F32, 1, "frstd")
            FS.rstd.append(t_)
            FS.B_rstd.append(b_[0])
        FS.rr = 0

        def fload(dst, src_, wr, cost):
            kb.dma(pool, lambda e: e.dma_start(out=dst, in_=src_.rearrange("(k p) c -> p k c", p=128)), writes=wr,
                   cost=cost)
        fload(wgA, w_gate_d[:, 0:half], [B_wg[0]], 26.0)
        fload(wuA, w_up_d[:, 0:half], [B_wu[0]], 26.0)
        fload(wgB, w_gate_d[:, half:DFF], [B_wg[1]], 26.0)
        fload(wuB, w_up_d[:, half:DFF], [B_wu[1]], 26.0)
        fload(wd[:, 0:FH, :], w_down_d[0:FH * 128, :], [B_wd[0]], 26.0)
        fload(wd[:, FH:FC, :], w_down_d[FH * 128:DFF, :], [B_wd[1]], 26.0)
        out_tokens = []
        NFT = SEG // NF

        def f_load(t):
            hh, B_hh = h2t[t % 2], B_h2t[t % 2]
            kb.dma(sp, lambda e: e.dma_start(
                out=hh, in_=h2d[:, t * NF:(t + 1) * NF].rearrange("(k p) n -> p k n", p=128)), reads=[B_h2d[t]],
                writes=B_hh, cost=8.0)

        def f_prenorm(t):
            hh, B_hh = h2t[t % 2], B_h2t[t % 2]
            for _ in rms_stats(FS, lambda c: hh[:, c, :], B_hh, KC, ones1024, lambda c: fsqp[:, c, :], B_fsqp,
                               src_all=hh, sq_all=fsqp):
                pass
            r_, rb_ = FS.last_r
            for _ in rms_apply(lambda c: hh[:, c, :], B_hh, KC, 32, r_, rb_, lambda c: fxn[t % 2][:, c, :], B_fxn[t % 2]):
                pass

        f_load(0)
        f_prenorm(0)
        for t in range(NFT):
            hh, B_hh = h2t[t % 2], B_h2t[t % 2]
            xn_t, B_xn_t = fxn[t % 2], B_fxn[t % 2]
            for f in range(FC):
                psg, pbg = bank()
                hf_, fo_ = f // FH, (f % FH) * 128
                mm_group(psg[:, 0:NF], [(wgH[hf_][:, k, fo_:fo_ + 128], xn_t[:, k, :]) for k in range(KC)],
                         [B_wg[hf_]] + B_xn_t, [pbg])
                psu, pbu = bank()
                mm_group(psu[:, 0:NF], [(wuH[hf_][:, k, fo_:fo_ + 128], xn_t[:, k, :]) for k in range(KC)],
                         [B_wu[hf_]] + B_xn_t, [pbu])
                si = f % 2
                A(fsg[si], psg[:, 0:NF], AF.Silu, [pbg], [B_fsg[si]])
                TT(dve, fact[:, f, :], fsg[si], psu[:, 0:NF], ALU.mult, [B_fsg[si], pbu], [B_fact[f]])
                if t + 1 < NFT and f == 3:
                    f_load(t + 1)
                if t + 1 < NFT and f == 9:
                    f_prenorm(t + 1)
            for _ in proj_evac(FS, wd, B_wd, lambda k: fact[:, k, :], B_fact, 40,
                               lambda oc: fyg[:, oc, :], B_fyg, lambda c: fsq[:, c, :], B_fsq, nk=FC):
                pass
            for _ in postnorm_res(FS, lambda oc: hh[:, oc, :], B_hh, lambda oc: fyg[:, oc, :], B_fyg,
                                  lambda c: fsq[:, c, :], B_fsq):
                pass
            tok = kb.dma(sp, lambda e, t=t, hh=hh: e.dma_start(
                out=outT[:, t * NF:(t + 1) * NF].rearrange("(k p) n -> p k n", p=128), in_=hh), reads=B_hh, cost=8.0)
            out_tokens.append(tok)
        kb.join(sp, out_tokens)
        with nc.Block() as block:
            kb.emit(block)
    nc._kb = kb
    return nc


def _colvec(v, nchunk):
    return np.ascontiguousarray(np.asarray(v, np.float32).reshape(nchunk, 128).T)


def kernel(x, mem, w_in, ln_v_g, ln_v_b, w_s, b_s, conv_w, conv_b, w_a, b_a, w_x, b_x, lam,
           g_out_gmlp, g_out_lru, w_out, w_q, w_kv, w_o, w_gate, w_up, w_down,
           n_pre_mix, n_post_mix, n_pre_x, n_mem, n_post_x, n_pre_ffn, n_post_ffn):
    f = lambda a: np.asarray(a, np.float32)
    x, mem = f(x), f(mem)
    vecs = np.zeros((128, NV), np.float32)
    for col, v in ((0, n_pre_mix), (8, n_post_mix), (16, n_pre_x), (24, n_post_x), (32, n_pre_ffn),
                   (40, n_post_ffn), (48, n_mem)):
        vecs[:, col:col + 8] = _colvec(f(v)[0], 8)
    vecs[:, 56:60] = _colvec(f(g_out_gmlp)[0], 4)
    vecs[:, 60:64] = _colvec(f(g_out_lru)[0], 4)
    for k in range(4):
        vecs[:, 64 + 4 * k:68 + 4 * k] = _colvec(f(conv_w)[0, k], 4)
    vecs[:, 80:84] = _colvec(f(conv_b)[0], 4)
    vecs[:, 84:88] = _colvec(f(b_a)[0], 4)
    vecs[:, 88:92] = _colvec(f(b_x)[0], 4)
    vecs[:, 92:96] = _colvec(f(lam)[0], 4)
    lnrow = np.ascontiguousarray(np.broadcast_to(
        np.concatenate([f(ln_v_g)[0], f(ln_v_b)[0]])[None, :], (128, 1024)))
    wsT = np.ascontiguousarray(f(w_s)[0].transpose(2, 0, 1).reshape(128, 512))
    cmask = np.triu(np.ones((128, 128), np.float32))

    def blockdiag(w):
        w = f(w)[0]
        out = np.zeros((128, 4, 128), np.float32)
        for c in range(4):
            for hh in range(2):
                out[hh * 64:(hh + 1) * 64, c, hh * 64:(hh + 1) * 64] = w[2 * c + hh]
        return out.reshape(128, 512)
    wab, wxb = blockdiag(w_a), blockdiag(w_x)
    bsrow = np.ascontiguousarray(f(b_s)[0].reshape(1, 512))
    common = {
        "w_in": np.ascontiguousarray(f(w_in)[0]), "w_out": np.ascontiguousarray(f(w_out)[0]),
        "w_q": np.ascontiguousarray(f(w_q)[0]), "w_kv": np.ascontiguousarray(f(w_kv)[0]),
        "w_o": np.ascontiguousarray(f(w_o)[0]), "w_gate": np.ascontiguousarray(f(w_gate)[0]),
        "w_up": np.ascontiguousarray(f(w_up)[0]), "w_down": np.ascontiguousarray(f(w_down)[0]),
        "vecs": vecs, "lnrow": lnrow, "wsT": wsT, "cmask": cmask, "wab": wab, "wxb": wxb, "bsrow": bsrow,
    }
    in_maps = []
    for c in range(NCORES):
        b, j = c // 4, c % 4
        xsT = np.zeros((D, SEQ), np.float32)
        npre = SEG * j
        xsT[:, SEQ - SEG - npre:] = x[b, 0:npre + SEG, :].T
        tm = np.zeros((128, N_PRE), np.float32)
        tm[:, N_PRE - 4 * j:] = 1.0
        m = dict(common)
        m["xsT"] = xsT
        m["memT"] = np.ascontiguousarray(mem[b].T)
        m["tmask"] = tm
        in_maps.append(m)
    nc = build_program()
    res = run_bass_kernel_spmd(nc, in_maps, core_ids=list(range(NCORES)))
    out = np.empty((2, SEQ, D), np.float32)
    for c in range(NCORES):
        b, j = c // 4, c % 4
        out[b, j * SEG:(j + 1) * SEG, :] = res.results[c]["outT"].T
    return out
```

```python
from contextlib import ExitStack
import numpy as np
import concourse.bass as bass
import concourse.mybir as mybir
from concourse.bass_utils import run_bass_kernel_spmd

F32 = mybir.dt.float32
BF16 = mybir.dt.bfloat16
U8 = mybir.dt.uint8
ALU = mybir.AluOpType
AF = mybir.ActivationFunctionType

NCORES = 8
D = 1024
KC = 8
SEQ = 8192
SEG = 2048
NT = 512
NF = 256
N_PRE = 12
N_OWN = 4
DFF = 2816
FC = 22
EPS = 1e-6
NV = 96
LAG_P = 4
LAG_M = 10
FINE = 1


class Buf:
    __slots__ = ("name", "writer", "readers", "rng", "active")
    registry = []

    def __init__(self, name="", writer=None, rng=None):
        self.name = name
        self.writer = writer
        self.readers = []
        self.rng = rng
        self.active = False
        Buf.registry.append(self)


class Op:
    __slots__ = ("id", "eng", "fn", "deps", "cost", "is_dma", "tset", "tag", "start", "done", "count", "sem")

    def __init__(self, id_, eng, fn, deps, cost, is_dma, tset, tag):
        self.id, self.eng, self.fn, self.deps, self.cost = id_, eng, fn, deps, cost
        self.is_dma, self.tset, self.tag = is_dma, tset, tag
        self.start = self.done = None
        self.count = None
        self.sem = None


class Eng:
    def __init__(self, name, sem, selfsync):
        self.name = name
        self.sem = sem
        self.selfsync = selfsync
        self.order = []


class KB:
    def __init__(self, nc, stack, n_dma_sems=20):
        self.nc = nc
        mk = lambda name: stack.enter_context(nc.semaphore(name))
        self.pe = Eng("pe", mk("s_pe"), False)
        self.act = Eng("act", mk("s_act"), True)
        self.dve = Eng("dve", mk("s_dve"), True)
        self.pool = Eng("pool", mk("s_pool"), True)
        self.sp = Eng("sp", mk("s_sp"), False)
        self.engs = [self.pe, self.act, self.dve, self.pool, self.sp]
        self.dma_pools = {"sp": [mk(f"s_dsp{i}") for i in range(n_dma_sems)],
                          "pool": [mk(f"s_dpl{i}") for i in range(12)]}
        self.ops = []
        self.ranged = []
        self.makespan = None

    def _new(self, eng, fn, reads, writes, cost, is_dma, tset, tag, extra_deps):
        deps = set(extra_deps)
        for b in list(reads) + list(writes):
            if b.rng is not None and not b.active:
                b.active = True
                for o in self.ranged:
                    if o is not b and o.rng[0] < b.rng[1] and b.rng[0] < o.rng[1]:
                        if o.writer is not None:
                            deps.add(o.writer)
                        deps.update(o.readers)
                self.ranged.append(b)
        for b in reads:
            if b.writer is not None:
                deps.add(b.writer)
        for b in writes:
            if b.writer is not None:
                deps.add(b.writer)
            deps.update(b.readers)
        op = Op(len(self.ops), eng, fn, deps, cost, is_dma, tset, tag)
        self.ops.append(op)
        for b in reads:
            b.readers.append(op.id)
        for b in writes:
            b.writer = op.id
            b.readers = []
        return op.id

    def op(self, eng, fn, reads=(), writes=(), cost=0.5, tag="", tset=None, extra_deps=()):
        return self._new(eng, fn, reads, writes, cost, False, tset, tag, extra_deps)

    def dma(self, eng, fn, reads=(), writes=(), cost=10.0, tag="dma", extra_deps=()):
        return self._new(eng, fn, reads, writes, cost, True, None, tag, extra_deps)

    def barrier(self, eng):
        return self._new(eng, None, (), (), 0.0, False, None, "barrier", range(len(self.ops)))

    def join(self, eng, deps):
        return self._new(eng, None, (), (), 0.0, False, None, "join", deps)

    def schedule(self):
        ops = self.ops
        n = len(ops)
        succ = [[] for _ in range(n)]
        unmet = [0] * n
        for o in ops:
            o.deps.discard(o.id)
            unmet[o.id] = len(o.deps)
            for d in o.deps:
                succ[d].append(o.id)
        ready_t = [0.0] * n
        ready = {e.name: [] for e in self.engs}
        for o in ops:
            if unmet[o.id] == 0:
                ready[o.eng.name].append(o.id)
        tfree = {e.name: 0.0 for e in self.engs}
        last_tset = [None]
        remaining = n
        for e in self.engs:
            e.order = []
        while remaining:
            best = None
            for e in self.engs:
                rl = ready[e.name]
                if not rl:
                    continue
                tf = tfree[e.name]
                cand = None
                for oid in rl:
                    st = max(tf, ready_t[oid])
                    key = st
                    o = ops[oid]
                    if e.name == "act" and o.tset and last_tset[0] and o.tset != last_tset[0]:
                        key += 0.5
                    k = (key, oid)
                    if cand is None or k < cand[0]:
                        cand = (k, oid, st)
                if best is None or (cand[2], cand[1]) < (best[2], best[1]):
                    best = cand + (e,)
            _, oid, st, e = best
            o = ops[oid]
            ready[e.name].remove(oid)
            if e.name == "act" and o.tset:
                if last_tset[0] and o.tset != last_tset[0]:
                    st += 1.3
                last_tset[0] = o.tset
            o.start = st
            if o.is_dma:
                tfree[e.name] = st + 0.1
                o.done = st + o.cost
            else:
                tfree[e.name] = st + o.cost
                o.done = st + o.cost
            e.order.append(oid)
            remaining -= 1
            for s in succ[oid]:
                unmet[s] -= 1
                rt_ = o.done + (0.3 if ops[s].eng is not e else 0.1)
                if ready_t[s] < rt_:
                    ready_t[s] = rt_
                if unmet[s] == 0:
                    ready[ops[s].eng.name].append(s)
        self.makespan = max(o.done for o in ops)
        return self.makespan

    def emit(self, block):
        ops = self.ops
        if self.makespan is None:
            self.schedule()
        for e in self.engs:
            cnt = 0
            rr = 0
            slots = {}
            for oid in e.order:
                o = ops[oid]
                if o.fn is None:
                    continue
                if o.is_dma:
                    sems_ = self.dma_pools[e.name]
                    k = rr % len(sems_)
                    rr += 1
                    prev = slots.get(k, (0, None))
                    o.sem, o.count = sems_[k], prev[0] + 16
                    slots[k] = (o.count, oid)
                else:
                    cnt += 1
                    o.sem, o.count = e.sem, cnt
        none_need = {}

        def merge(dst, key, v, s):
            if dst.get(key, (0, None))[0] < v:
                dst[key] = (v, s)

        def need_of_none(o):
            if o.id in none_need:
                return none_need[o.id]
            need = {}
            if o.tag == "barrier":
                for od in ops[:o.id]:
                    if od.fn is not None:
                        merge(need, id(od.sem), od.count, od.sem)
            else:
                for d in o.deps:
                    od = ops[d]
                    if od.fn is None:
                        for key, (v, s) in need_of_none(od).items():
                            merge(need, key, v, s)
                    else:
                        merge(need, id(od.sem), od.count, od.sem)
            none_need[o.id] = need
            return need

        plans = {}
        for e in self.engs:
            seen = {}
            plan = []
            rr = 0
            slot_prev = {}
            for oid in e.order:
                o = ops[oid]
                need = {}
                if o.fn is None:
                    need = dict(need_of_none(o))
                else:
                    for d in o.deps:
                        od = ops[d]
                        if od.fn is None:
                            for key, (v, s) in need_of_none(od).items():
                                merge(need, key, v, s)
                            continue
                        if od.eng is e and not od.is_dma and not e.selfsync:
                            continue
                        merge(need, id(od.sem), od.count, od.sem)
                if o.is_dma:
                    sems_ = self.dma_pools[e.name]
                    k = rr % len(sems_)
                    rr += 1
                    if k in slot_prev:
                        merge(need, id(o.sem), slot_prev[k], o.sem)
                    slot_prev[k] = o.count
                waits = []
                for key, (v, s) in need.items():
                    if seen.get(key, 0) >= v:
                        continue
                    seen[key] = v
                    waits.append((s, v))
                plan.append((waits, o))
            plans[e.name] = plan

        def run(eng):
            def body(e_):
                for waits, o in plans[eng.name]:
                    for s, v in waits:
                        e_.wait_ge(s, v)
                    if o.fn is None:
                        continue
                    ins = o.fn(e_)
                    ins.then_inc(o.sem, 16 if o.is_dma else 1)
            return body
        block.tensor(run(self.pe))
        block.scalar(run(self.act))
        block.vector(run(self.dve))
        block.gpsimd(run(self.pool))
        block.sync(run(self.sp))


class Arena:
    def __init__(self, mem, size):
        self.mem = mem
        self.size = size
        self.off = 0

    def alloc(self, free_shape, dtype, name=""):
        esz = 2 if dtype == BF16 else 4
        n = 1
        for s in free_shape:
            n *= s
        nbytes = (n * esz + 63) // 64 * 64
        assert self.off + nbytes <= self.size, (name, self.off, nbytes, self.size)
        ap = self.mem[:, self.off:self.off + n * esz].bitcast(dtype)
        self.off += nbytes
        if len(free_shape) == 2:
            ap = ap.rearrange("p (a b) -> p a b", a=free_shape[0])
        return ap


def bufs(n, name):
    return [Buf(f"{name}{i}") for i in range(n)]


def build_program():
    nc = bass.Bass("TRN2", target_bir_lowering=False)

    def din(name, shape):
        return nc.dram_tensor(name, shape, F32, kind="ExternalInput").ap()
    xsT = din("xsT", [D, SEQ])
    memT = din("memT", [D, 256])
    w_in_d = din("w_in", [D, 2048])
    w_out_d = din("w_out", [D, D])
    w_q_d = din("w_q", [D, D])
    w_kv_d = din("w_kv", [D, 2048])
    w_o_d = din("w_o", [D, D])
    w_gate_d = din("w_gate", [D, DFF])
    w_up_d = din("w_up", [D, DFF])
    w_down_d = din("w_down", [DFF, D])
    vecs_d = din("vecs", [128, NV])
    lnrow_d = din("lnrow", [128, 1024])
    wsT_d = din("wsT", [128, 512])
    cmask_d = din("cmask", [128, 128])
    wab_d = din("wab", [128, 512])
    wxb_d = din("wxb", [128, 512])
    bsrow_d = din("bsrow", [1, 512])
    tmask_d = din("tmask", [128, N_PRE])
    outT = nc.dram_tensor("outT", [D, SEG], F32, kind="ExternalOutput").ap()
    h2d = nc.dram_tensor("h2d", [D, SEG], F32).ap()
    h1d = nc.dram_tensor("h1d", [D, SEG], F32).ap()

    with ExitStack() as st:
        TOTAL = 211968
        mem = st.enter_context(nc.sbuf_tensor("mem", [128, TOTAL], U8))
        psum = [st.enter_context(nc.psum_tensor(f"ps{i}", [128, 512], F32)) for i in range(8)]
        psb = bufs(8, "ps")
        kb = KB(nc, st)
        pe, act, dve, pool, sp = kb.pe, kb.act, kb.dve, kb.pool, kb.sp
        bank_rr = [0]

        def bank():
            i = bank_rr[0] % 8
            bank_rr[0] += 1
            return psum[i], psb[i]

        ar = Arena(mem, TOTAL)
        vecs = ar.alloc([NV], F32, "vecs")
        ones1024 = ar.alloc([128], BF16)
        ones1 = ar.alloc([128], BF16)
        B_vecs, B_const = Buf("vecs"), Buf("const")
        F_BASE = ar.off

        def ralloc(shape, dtype, name):
            off = ar.off
            ap = ar.alloc(shape, dtype, name)
            return ap, Buf(name, rng=(off, ar.off))
        der, B_der = ralloc([32], F32, "der")
        tmask, B_tmask = ralloc([N_PRE], F32, "tmask")
        state_off = ar.off
        state = ar.alloc([4], F32, "state")
        B_state = [Buf("state%d" % i, rng=(state_off, ar.off)) for i in range(4)]
        ones512, B_c512 = ralloc([128], BF16, "ones512")
        lng_off = ar.off
        lng = ar.alloc([512], F32)
        lnb = ar.alloc([512], F32)
        B_ln = Buf("ln", rng=(lng_off, ar.off))
        wsm, B_wsm = ralloc([4, 128], BF16, "wsm")
        wab_off = ar.off
        wab = ar.alloc([4, 128], BF16)
        wxb = ar.alloc([4, 128], BF16)
        B_wab = Buf("wab", rng=(wab_off, ar.off))
        bsrow, B_bs = ralloc([512], BF16, "bsrow")
        stt, B_stt = ralloc([40], F32, "stt")
        PB_END = ar.off

        def alias(off, free_shape, dtype):
            esz = 2 if dtype == BF16 else 4
            n = 1
            for s_ in free_shape:
                n *= s_
            ap = mem[:, off:off + n * esz].bitcast(dtype)
            if len(free_shape) == 2:
                ap = ap.rearrange("p (a b) -> p a b", a=free_shape[0])
            return ap

        class Set:
            pass

        def rbufs(off, nbytes, nb, name):
            step = nbytes // nb
            return [Buf("%s%d" % (name, i), rng=(off + i * step, off + (i + 1) * step)) for i in range(nb)]

        def allocb(shape, dtype, nb, name=""):
            off = ar.off
            ap = ar.alloc(shape, dtype, name)
            n = 1
            for s_ in shape:
                n *= s_
            return ap, rbufs(off, n * (2 if dtype == BF16 else 4), nb, name), off

        def make_set(name, N, kind, xrh=None, B_xrh=None):
            S = Set()
            S.N, S.NC, S.name = N, N // 128, name
            S.xrh, S.B_xrh = xrh, B_xrh
            S.xt, S.B_xt, _ = allocb([KC, N], F32, 8, "xt")
            S.xn, S.B_xn, xn_off = allocb([KC, N], BF16, 8, "xn")
            S.sq, S.B_sq, sq_off = allocb([KC, N], BF16, 8, "sq")
            nslot = 1 if kind == "P" else 2
            mse_, bm_, _ = allocb([N], F32, 1, "mse")
            S.mse, S.B_mse = mse_, bm_[0]
            S.rstd, S.B_rstd = [], []
            for _i in range(nslot):
                r_, br_, _ = allocb([N], F32, 1, "rstd")
                S.rstd.append(r_)
                S.B_rstd.append(br_[0])
            S.rr = 0
            if kind == "A":
                S.yg, S.B_yg, yg_off = allocb([KC, N], F32, 8, "yg")
                S.qT = alias(yg_off, [KC, N], BF16)
                S.B_qT = [S.B_yg[j // 2] for j in range(8)]
                S.rc = [alias(yg_off + KC * N * 2 + r_ * N * 4, [N], F32) for r_ in range(2)]
                S.B_rc = [[S.B_yg[4 + r_ * (N * 4 // (N * 4))]] for r_ in range(2)]
                return S
            S.hs, S.B_hs, _ = allocb([4, N], F32, 4, "hs")
            S.thi4, S.thr4 = alias(xn_off, [4, N], F32), alias(sq_off, [4, N], F32)
            S.B_thi4 = [[S.B_xn[2 * c], S.B_xn[2 * c + 1]] for c in range(4)]
            S.B_thr4 = [[S.B_sq[2 * c], S.B_sq[2 * c + 1]] for c in range(4)]
            if kind != "P":
                xcb_, S.B_xcb4, _ = allocb([4, N], BF16, 4, "xcb")
                S.xcb4 = [xcb_[:, c, :] for c in range(4)]
                S.xcb = xcb_
            if kind == "P":
                S.xb, S.B_xb, _ = allocb([KC, N], BF16, 8, "xb")
                S.tm = S.xt[:, 4:8, :]
                S.ta4 = [S.xt[:, c, :] for c in range(4)]
                S.B_ta4 = [[S.B_xt[c]] for c in range(4)]
                S.tm4 = [S.xt[:, 4 + c, :] for c in range(4)]
                S.B_tm4 = [S.B_xt[4 + c] for c in range(4)]
            else:
                tm_, S.B_tm4, _ = allocb([4, N], F32, 4, "tm")
                S.tm4 = [tm_[:, c, :] for c in range(4)]
                S.tm = tm_
                S.gu, S.B_gu, _ = allocb([4, N], F32, 4, "gu")
                S.ggr, S.B_ggr, _ = allocb([4, N], F32, 4, "ggr")
                S.gv, S.B_gv, gv_off = allocb([S.NC, 512], F32, S.NC, "gv")
                S.vtm, S.B_vtm, _ = allocb([S.NC, 512], BF16, S.NC, "vtm")
                gvv = alias(gv_off, [4, N], F32)
                S.ta4 = [gvv[:, i, :] for i in range(4)]
                S.B_ta4 = [[S.B_gv[i * S.NC // 4]] for i in range(4)]
            return S

        NM = 256
        XW = 2 * (NM + 4)
        xrh_off = [ar.off, ar.off + 4 * XW * 4]
        xrhP = [ar.alloc([4, XW], F32, "xrh0"), ar.alloc([4, XW], F32, "xrh1")]

        def xrh_bufs(i, c0, c1, name):
            return [Buf("%s%d" % (name, c), rng=(xrh_off[i] + (c * XW + c0) * 4, xrh_off[i] + (c * XW + c1) * 4))
                    for c in range(4)]
        R0 = ar.off
        w_in, B_win1, win_off = allocb([KC, 2048], BF16, 1, "w_in")
        win_rng = B_win1[0].rng
        B_win = [Buf("w_in%d" % j, rng=win_rng) for j in range(4)]
        W_BASE = ar.off
        ar.off = R0
        S0 = make_set("p0", NT, "P", xrhP[0][:, :, 0:NT + 4], xrh_bufs(0, 0, NT + 4, "xrhp0"))
        S2 = make_set("p2", NT, "P")
        S1 = make_set("p1", NT, "P", xrhP[1][:, :, 0:NT + 4], xrh_bufs(1, 0, NT + 4, "xrhp1"))
        x2_off = ar.off
        xrh2 = ar.alloc([4, XW], F32, "xrh2")
        S2.xrh = xrh2[:, :, 0:NT + 4]
        S2.B_xrh = [Buf("xrhp2%d" % c, rng=(x2_off + c * XW * 4, x2_off + (c * XW + NT + 4) * 4)) for c in range(4)]
        w_xr, bwx_, _ = allocb([KC, 512], BF16, 1, "w_xr")
        B_wxr = bwx_[0]
        wab32, b32a_, _ = allocb([4, 128], F32, 1, "wab32")
        wxb32, b32x_, _ = allocb([4, 128], F32, 1, "wxb32")
        B_w32 = [b32a_[0], b32x_[0]]
        PS = [S0, S2, S1]
        P_END = ar.off
        ar.off = W_BASE
        set_off = []
        MS = []
        xv = [(1, 0, NM + 4), (0, 0, NM + 4), (0, NM + 4, XW)]
        for i in range(3):
            set_off.append(ar.off)
            ii, c0, c1 = xv[i]
            MS.append(make_set("m%d" % i, NM, "M", xrhP[ii][:, :, c0:c1], xrh_bufs(ii, c0, c1, "xrhm%d" % i)))
        set_off.append(ar.off)
        w_out, bwo_, _ = allocb([KC, D], BF16, 1, "w_out")
        B_wout = bwo_[0]
        assert ar.off <= TOTAL
        B_h1d = bufs(SEG // NM, "h1d")
        B_h2d = bufs(SEG // NM, "h2d")
        ar.off = win_off
        wkvK, bk_, _ = allocb([KC, D], BF16, 1, "wkvK")
        wkvV, bv_, _ = allocb([KC, D], BF16, 1, "wkvV")
        B_wkvK, B_wkvV = bk_[0], bv_[0]
        ar.off = xrh_off[0]
        mt, B_mt, _ = allocb([KC, 256], F32, 8, "mt")
        mn, B_mn, _ = allocb([KC, 256], BF16, 8, "mn")
        msq, B_msq, _ = allocb([KC, 256], BF16, 8, "msq")
        assert ar.off <= R0
        ar.off = set_off[2]
        w_q, bq_, _ = allocb([KC, D], BF16, 1, "w_q")
        w_o, bo_, _ = allocb([KC, D], BF16, 1, "w_o")
        kvt, bkv_, _ = allocb([4096], BF16, 1, "kvt")
        B_wq, B_wo, B_kT = bq_[0], bo_[0], bkv_[0]
        B_vv = B_kT
        kT = kvt[:, 0:2048].rearrange("p (a b) -> p a b", a=8)
        vv = kvt[:, 2048:4096].rearrange("p (a b) -> p a b", a=2)
        KS = Set()
        KS.N = 256
        kmse_, bkm_, _ = allocb([256], F32, 1, "kmse")
        krs_, bkr_, _ = allocb([256], F32, 1, "krstd")
        KS.mse, KS.B_mse, KS.rstd, KS.B_rstd, KS.rr = kmse_, bkm_[0], [krs_], [bkr_[0]], 0
        assert ar.off <= set_off[3], (ar.off, set_off)
        ar.off = set_off[0]
        AS = [make_set("a%d" % i, NM, "A") for i in range(3)]
        assert ar.off <= set_off[2], (ar.off, set_off)

        def V(col):
            return vecs[:, col:col + 1]

        def mm_group(out_ap, pairs, reads, writes, f32=False):
            n = len(pairs) * (4 if f32 else 1)

            npair = len(pairs)

            def fn(e):
                ins = None
                for i, (l, r) in enumerate(pairs):
                    ins = e.matmul(out_ap, lhsT=l, rhs=r, start=(i == 0), stop=(i == npair - 1))
                return ins
            nmov = pairs[0][1].free_size()
            per = {512: 0.285, 256: 0.118}.get(nmov, 0.11 + nmov * 0.0004)
            kb.op(pe, fn, reads, writes, cost=n * per, tag='mm%dx%d' % (n, nmov))

        def A(out, in_, func, reads, writes, **kw):
            nm = str(func).split('.')[-1]
            tset = {"Gelu_apprx_tanh": "g", "Tanh": "g", "Exp": "e", "Ln": "e", "Silu": "s"}.get(nm)
            kb.op(act, lambda e: e.activation(out=out, in_=in_, func=func, **kw), reads, writes,
                  cost=0.2 + out.free_size() * 0.00085, tag=nm, tset=tset)

        def dcost(out, f=1.0):
            return 0.08 + out.free_size() * 0.00105 * f

        def TT(eng, out, in0, in1, op, reads, writes):
            kb.op(eng, lambda e: e.tensor_tensor(out=out, in0=in0, in1=in1, op=op), reads, writes, cost=dcost(out), tag='tt')

        def TS(eng, out, in0, s1, s2, op0, op1, reads, writes):
            if op1 is None:
                kb.op(eng, lambda e: e.tensor_scalar(out=out, in0=in0, scalar1=s1, scalar2=None, op0=op0), reads, writes,
                      cost=dcost(out, 0.7), tag='ts')
            else:
                kb.op(eng, lambda e: e.tensor_scalar(out=out, in0=in0, scalar1=s1, scalar2=s2, op0=op0, op1=op1),
                      reads, writes, cost=dcost(out, 0.7), tag='ts')

        def STT(out, in0, scalar, in1, op0, op1, reads, writes):
            kb.op(dve, lambda e: e.scalar_tensor_tensor(out=out, in0=in0, scalar=scalar, in1=in1, op0=op0, op1=op1),
                  reads, writes, cost=dcost(out, 1.25), tag='stt')

        def stats_rstd(S, sq_aps, sq_bufs, ones_t, cb=None):
            n = S.N
            i = S.rr % len(S.rstd)
            S.rr += 1
            ps, pb = bank()
            mm_group(ps[:, 0:n], [(ones_t[:, :], s) for s in sq_aps], list(sq_bufs) + [cb or B_const], [pb])
            A(S.mse, ps[:, 0:n], AF.Ln, [pb], [S.B_mse], bias=EPS)
            A(S.rstd[i], S.mse, AF.Exp, [S.B_mse], [S.B_rstd[i]], scale=-0.5)
            return S.rstd[i], S.B_rstd[i]

        def rms_stats(S, src, src_b, C, ones_t, sqt, sq_b, cb=None, src_all=None, sq_all=None):
            if src_all is not None:
                A(sq_all, src_all, AF.Square, list(src_b[0:C]), list(sq_b[0:C]))
                yield
            else:
                for c in range(C):
                    A(sqt(c), src(c), AF.Square, [src_b[c]], [sq_b[c]])
                    if c % FINE == FINE - 1:
                        yield
            S.last_r = stats_rstd(S, [sqt(c) for c in range(C)], [sq_b[c] for c in range(C)], ones_t, cb)
            yield

        def rms_apply(src, src_b, C, gcol, r, rb, dst, dst_b):
            for c in range(C):
                STT(dst(c), src(c), V(gcol + c), r, ALU.mult, ALU.mult, [src_b[c], rb, B_vecs], [dst_b[c]])
                if c % FINE == FINE - 1:
                    yield

        def proj_evac(S, w_sb, w_b, rhs, rhs_b, gcol, yg, yg_b, sqt, sq_b, nk=KC):
            n = S.N
            for oc in range(KC):
                ps, pb = bank()
                mm_group(ps[:, 0:n], [(w_sb[:, k, oc * 128:(oc + 1) * 128], rhs(k)) for k in range(nk)],
                         list(w_b) + [rhs_b[k] for k in range(nk)], [pb])
                A(yg(oc), ps[:, 0:n], AF.Identity, [pb, B_vecs], [yg_b[oc]], scale=V(gcol + oc))
                A(sqt(oc), ps[:, 0:n], AF.Square, [pb], [sq_b[oc]])
                if oc % FINE == FINE - 1:
                    yield

        def postnorm_res(S, res, res_b, yg, yg_b, sqt, sq_b):
            r, rb = stats_rstd(S, [sqt(c) for c in range(KC)], sq_b, ones1024)
            yield
            for oc in range(KC):
                TT(dve, yg(oc), yg(oc), r, ALU.mult, [yg_b[oc], rb], [yg_b[oc]])
                TT(dve, res(oc), res(oc), yg(oc), ALU.add, [res_b[oc], yg_b[oc]], [res_b[oc]])
                if oc % FINE == FINE - 1:
                    yield

        def wload(dst, src, wr, extra=()):
            kb.dma(pool, lambda e: e.dma_start(out=dst, in_=src.rearrange("(k p) c -> p k c", p=128)), writes=wr,
                   cost=3.0 + dst.free_size() * 128 * 4 / 250e3, extra_deps=extra)

        kb.dma(sp, lambda e: e.dma_start(out=vecs, in_=vecs_d), writes=[B_vecs])
        kb.dma(sp, lambda e: e.dma_start(out=tmask, in_=tmask_d), writes=[B_tmask])
        kb.op(pool, lambda e: e.memset(ones1024, 1.0 / 1024.0), writes=[B_const])
        kb.op(pool, lambda e: e.memset(ones512, 1.0 / 512.0), writes=[B_c512])
        kb.op(pool, lambda e: e.memset(ones1, 1.0), writes=[B_const])
        kb.op(pool, lambda e: e.memset(state, 0.0), writes=B_state)
        kb.op(pool, lambda e: e.memset(xrhP[0], 0.0), writes=S0.B_xrh)
        kb.op(pool, lambda e: e.memset(xrhP[1], 0.0), writes=S1.B_xrh)
        kb.op(pool, lambda e: e.memset(xrh2, 0.0), writes=S2.B_xrh)
        kb.dma(sp, lambda e: e.dma_start(out=S0.xt, in_=w_in_d[:, 1024:1536].rearrange("(k p) c -> p k c", p=128)),
               writes=S0.B_xt, cost=10.0)
        for k_ in range(KC):
            TS(dve, w_xr[:, k_, :], S0.xt[:, k_, :], V(k_), None, ALU.mult, None, [S0.B_xt[k_], B_vecs], [B_wxr])
        kb.dma(sp, lambda e: e.dma_start(out=wab32.rearrange("p a b -> p (a b)"), in_=wab_d), writes=[B_w32[0]], cost=3.0)
        kb.dma(sp, lambda e: e.dma_start(out=wxb32.rearrange("p a b -> p (a b)"), in_=wxb_d), writes=[B_w32[1]], cost=3.0)
        TS(dve, der[:, 0:4], vecs[:, 84:88], 0.5, None, ALU.mult, None, [B_vecs], [B_der])
        TS(dve, der[:, 4:8], vecs[:, 88:92], 0.5, None, ALU.mult, None, [B_vecs], [B_der])
        A(der[:, 16:20], vecs[:, 92:96], AF.Exp, [B_vecs], [B_der], scale=-1.0)
        A(der[:, 20:24], der[:, 16:20], AF.Ln, [B_der], [B_der], bias=1.0)
        TS(dve, der[:, 12:16], der[:, 20:24], -4.0, None, ALU.mult, None, [B_der], [B_der])
        TS(dve, der[:, 8:12], der[:, 20:24], -8.0, None, ALU.mult, None, [B_der], [B_der])

        def load_x(tok0, S, extra=()):
            return kb.dma(sp, lambda e: e.dma_start(
                out=S.xt, in_=xsT[:, tok0:tok0 + S.N].rearrange("(k p) n -> p k n", p=128)), writes=S.B_xt,
                cost=3.0 + S.N * 0.014, extra_deps=extra)

        CL = 0.9999999
        LNH = -0.6931471805599453

        def xr_proj(S, Snext, compact=False, post=None):
            N = S.N
            for c in range(4):
                ps, pb = bank()
                if compact:
                    prs = [(w_xr[:, k, c * 128:(c + 1) * 128], S.xb[:, k, :]) for k in range(KC)]
                    wb_, xb_ = B_wxr, S.B_xb
                else:
                    prs = [(w_in[:, k, 1024 + c * 128:1024 + (c + 1) * 128], S.xn[:, k, :]) for k in range(KC)]
                    wb_, xb_ = B_win[2], S.B_xn
                mm_group(ps[:, 0:N], prs, [wb_] + xb_, [pb])
                if post is not None:
                    TT(dve, S.xrh[:, c, 3:3 + N], ps[:, 0:N], post[0], ALU.mult, [pb, post[1]], [S.B_xrh[c]])
                else:
                    A(S.xrh[:, c, 3:3 + N], ps[:, 0:N], AF.Copy, [pb], [S.B_xrh[c]])
                if Snext is not S:
                    A(Snext.xrh[:, c, 0:3], S.xrh[:, c, N:N + 3], AF.Copy, [S.B_xrh[c]], [Snext.B_xrh[c]])
                if c % FINE == FINE - 1:
                    yield

        def lru_conv(S, same_set_halo):
            N = S.N
            for c in range(4):
                xc, xr = S.hs[:, c, :], S.xrh[:, c, :]
                bxr, bxc = S.B_xrh[c], S.B_hs[c]
                TS(dve, xc, xr[:, 3:3 + N], V(64 + 12 + c), V(80 + c), ALU.mult, ALU.add, [bxr, B_vecs], [bxc])
                for k in range(3):
                    STT(xc, xr[:, k:k + N], V(64 + 4 * k + c), xc, ALU.mult, ALU.add, [bxr, bxc, B_vecs], [bxc])
                if c % FINE == FINE - 1:
                    yield
            if same_set_halo:
                A(S.xrh[:, :, 0:3], S.xrh[:, :, N:N + 3], AF.Copy, list(S.B_xrh), list(S.B_xrh))
            if S.N != NT:
                A(S.xcb, S.hs, AF.Copy, list(S.B_hs), list(S.B_xcb4))
            yield

        def lru_gates(S):
            N = S.N
            pss = []
            for c in range(4):
                psa, pba = bank()
                psx, pbx = bank()
                if N == NT:
                    mm_group(psa[:, 0:N], [(wab32[:, c, :], S.hs[:, c, :])], [B_w32[0], S.B_hs[c]], [pba], f32=True)
                    mm_group(psx[:, 0:N], [(wxb32[:, c, :], S.hs[:, c, :])], [B_w32[1], S.B_hs[c]], [pbx], f32=True)
                else:
                    mm_group(psa[:, 0:N], [(wab[:, c, :], S.xcb4[c])], [B_wab, S.B_xcb4[c]], [pba])
                    mm_group(psx[:, 0:N], [(wxb[:, c, :], S.xcb4[c])], [B_wab, S.B_xcb4[c]], [pbx])
                A(S.thr4[:, c, :], psa[:, 0:N], AF.Tanh, [pba, B_der], S.B_thr4[c], scale=0.5, bias=der[:, c:c + 1])
                A(S.thi4[:, c, :], psx[:, 0:N], AF.Tanh, [pbx, B_der], S.B_thi4[c], scale=0.5, bias=der[:, 4 + c:5 + c])
                if c % FINE == FINE - 1:
                    yield

        def lru_b1(S):
            for c in range(4):
                kq = der[:, 12 + c:13 + c]
                kq2 = der[:, 8 + c:9 + c]
                A(S.ta4[c], S.thr4[:, c, :], AF.Exp, S.B_thr4[c] + [B_der], S.B_ta4[c], scale=kq, bias=kq)
                A(S.tm4[c], S.thr4[:, c, :], AF.Exp, S.B_thr4[c] + [B_der], [S.B_tm4[c]], scale=kq2, bias=kq2)
            yield
            TS(dve, S.tm, S.tm, CL, None, ALU.min, None, list(S.B_tm4), list(S.B_tm4))
            yield

        def lru_b2(S):
            B_thi_all = [b_ for l_ in S.B_thi4 for b_ in l_]
            A(S.tm, S.tm, AF.Ln, list(S.B_tm4), list(S.B_tm4), scale=-1.0, bias=1.0)
            A(S.tm, S.tm, AF.Exp, list(S.B_tm4), list(S.B_tm4), scale=0.5, bias=LNH)
            yield
            STT(S.thi4, S.thi4, 1.0, S.hs, ALU.add, ALU.mult, B_thi_all + list(S.B_hs), B_thi_all)
            TT(dve, S.thi4, S.thi4, S.tm, ALU.mult, B_thi_all + list(S.B_tm4), B_thi_all)
            yield

        def lru_scan(S, tmask_col, own):
            N = S.N
            for c in range(4):
                kb.op(dve, lambda e, c=c: e.tensor_tensor_scan(out=S.hs[:, c, :], data0=S.ta4[c], data1=S.thi4[:, c, :],
                                                               initial=state[:, c:c + 1], op0=ALU.mult, op1=ALU.add),
                      S.B_ta4[c] + [B_state[c]] + S.B_thi4[c], [S.B_hs[c]], cost=0.1 + 0.0023 * N, tag="scan")
                if own:
                    kb.op(dve, lambda e, c=c: e.tensor_copy(out=state[:, c:c + 1], in_=S.hs[:, c, N - 1:N]),
                          [S.B_hs[c]], [B_state[c]], cost=0.1)
                    TT(dve, S.hs[:, c, :], S.hs[:, c, :], S.ggr[:, c, :], ALU.mult, [S.B_hs[c], S.B_ggr[c]], [S.B_hs[c]])
                else:
                    TT(dve, state[:, c:c + 1], S.hs[:, c, N - 1:N], tmask[:, tmask_col:tmask_col + 1], ALU.mult,
                       [S.B_hs[c], B_tmask], [B_state[c]])
                if c % FINE == FINE - 1:
                    yield

        def premix(S):
            yield from rms_stats(S, lambda c: S.xt[:, c, :], S.B_xt, KC, ones1024, lambda c: S.sq[:, c, :], S.B_sq,
                                 src_all=S.xt, sq_all=S.sq)
            r, rb = S.last_r
            yield from rms_apply(lambda c: S.xt[:, c, :], S.B_xt, KC, 0, r, rb, lambda c: S.xn[:, c, :], S.B_xn)

        def run_pipelined(make_gen, n_tiles, lag, hook=None, k=2):
            gens, count, nxt = [], {}, 0
            while nxt < n_tiles or gens:
                if nxt < n_tiles and (len(gens) == 0 or (len(gens) < k and count[gens[-1][0]] >= lag)):
                    gens.append((nxt, make_gen(nxt)))
                    count[nxt] = 0
                    nxt += 1
                    if hook is not None:
                        hook(nxt)
                for tt, g in list(gens):
                    try:
                        next(g)
                        count[tt] += 1
                    except StopIteration:
                        gens.remove((tt, g))

        p_loads = []

        def p_tile(t):
            S = PS[t % 3]
            last = (t == N_PRE - 1)
            Sn = S if last else PS[(t + 1) % 3]
            ex_ = list(p_loads) if t < 3 else []
            l1 = load_x(t * NT, S, ex_)
            l2 = kb.dma(pool, lambda e: e.dma_start(
                out=S.xb, in_=xsT[:, t * NT:(t + 1) * NT].rearrange("(k p) n -> p k n", p=128)), writes=S.B_xb,
                cost=12.0, extra_deps=ex_)
            p_loads[:] = [l1, l2]
            yield
            yield from rms_stats(S, lambda c: S.xt[:, c, :], S.B_xt, KC, ones1024, lambda c: S.sq[:, c, :], S.B_sq,
                                 src_all=S.xt, sq_all=S.sq)
            yield from xr_proj(S, Sn, compact=True, post=S.last_r)
            yield from lru_conv(S, last)
            yield from lru_gates(S)
            yield from lru_b1(S)
            yield from lru_b2(S)
            yield from lru_scan(S, t, False)

        def p_hook(nxt):
            if nxt == 2:
                kb.dma(sp, lambda e: e.dma_start(out=lng, in_=lnrow_d[:, 0:512]), writes=[B_ln])
                kb.dma(sp, lambda e: e.dma_start(out=lnb, in_=lnrow_d[:, 512:1024]), writes=[B_ln])
        run_pipelined(p_tile, N_PRE, LAG_P, p_hook, k=3)
        for j_ in (0, 3, 1, 2):
            wload(w_in[:, :, j_ * 512:(j_ + 1) * 512], w_in_d[:, j_ * 512:(j_ + 1) * 512], [B_win[j_]])
        wload(w_out, w_out_d, [B_wout])

        MB = MS[2]
        kb.dma(sp, lambda e: e.dma_start(out=MB.gv[:, 0, :], in_=wsT_d), writes=[MB.B_gv[0]])
        kb.dma(sp, lambda e: e.dma_start(out=MB.gv[:, 1, 0:128], in_=cmask_d), writes=[MB.B_gv[1]])
        for h in range(4):
            TT(dve, wsm[:, h, :], MB.gv[:, 0, h * 128:(h + 1) * 128], MB.gv[:, 1, 0:128], ALU.mult,
               [MB.B_gv[0], MB.B_gv[1]], [B_wsm])
        tC = MB.ggr[:, 0:2, :].rearrange("p a b -> p (a b)")
        tD = MB.ggr[:, 2:4, :].rearrange("p a b -> p (a b)")
        kb.dma(sp, lambda e: e.dma_start(out=tC, in_=wab_d), writes=MB.B_ggr[0:2])
        kb.dma(sp, lambda e: e.dma_start(out=tD, in_=wxb_d), writes=MB.B_ggr[2:4])
        kb.op(dve, lambda e: e.tensor_copy(out=wab.rearrange("p a b -> p (a b)"), in_=tC), MB.B_ggr[0:2], [B_wab])
        kb.op(dve, lambda e: e.tensor_copy(out=wxb.rearrange("p a b -> p (a b)"), in_=tD), MB.B_ggr[2:4], [B_wab])
        kb.dma(sp, lambda e: e.dma_start(out=MB.gu[0:1, 0:2, :].rearrange("p a b -> p (a b)"), in_=bsrow_d),
               writes=MB.B_gu[0:2])
        kb.op(dve, lambda e: e.tensor_copy(out=bsrow[0:1, :], in_=MB.gu[0:1, 0:2, :].rearrange("p a b -> p (a b)")),
              MB.B_gu[0:2], [B_bs])

        mv = stt[:, 24:32]
        rs4 = stt[:, 32:36]
        OWN0 = N_PRE * NT
        NMT = SEG // NM

        def mix_tile(i):
            S = MS[i % 3]
            Sn = MS[(i + 1) % 3]
            N, NC = S.N, S.NC
            xt, xn, sq, hs, gu, ggr = S.xt, S.xn, S.sq, S.hs, S.gu, S.ggr
            B_xt, B_xn, B_sq, B_hs, B_gu, B_ggr = S.B_xt, S.B_xn, S.B_sq, S.B_hs, S.B_gu, S.B_ggr
            yg = lambda oc: (gu[:, oc, :] if oc < 4 else hs[:, oc - 4, :])
            B_yg = B_gu + B_hs
            sqt = lambda c: sq[:, c, :]
            load_x(OWN0 + i * N, S)
            yield
            yield from premix(S)
            for c in range(4):
                ps, pb = bank()
                mm_group(ps[:, 0:N], [(w_in[:, k, c * 128:(c + 1) * 128], xn[:, k, :]) for k in range(KC)],
                         [B_win[0]] + B_xn, [pb])
                A(gu[:, c, :], ps[:, 0:N], AF.Gelu_apprx_tanh, [pb], [B_gu[c]])
                if c % FINE == FINE - 1:
                    yield
            for c in range(4):
                ps, pb = bank()
                mm_group(ps[:, 0:N], [(w_in[:, k, 1536 + c * 128:1536 + (c + 1) * 128], xn[:, k, :]) for k in range(KC)],
                         [B_win[3]] + B_xn, [pb])
                A(ggr[:, c, :], ps[:, 0:N], AF.Gelu_apprx_tanh, [pb], [B_ggr[c]])
                if c % FINE == FINE - 1:
                    yield
            for n in range(NC):
                ps, pb = bank()
                mm_group(ps[:, :], [(xn[:, k, n * 128:(n + 1) * 128], w_in[:, k, 512:1024]) for k in range(KC)],
                         [B_win[1]] + B_xn, [pb])
                A(S.gv[:, n, :], ps[:, :], AF.Gelu_apprx_tanh, [pb], [S.B_gv[n]])
                yield
            yield from xr_proj(S, Sn)
            yield from lru_conv(S, False)
            yield from lru_gates(S)
            for n in range(NC):
                kb.op(dve, lambda e, n=n: e.bn_stats(out=stt[:, 6 * n:6 * n + 6], in_=S.gv[:, n, :]), [S.B_gv[n]], [B_stt])
                kb.op(dve, lambda e, n=n: e.bn_aggr(out=mv[:, 2 * n:2 * n + 2], in_=stt[:, 6 * n:6 * n + 6]),
                      [B_stt], [B_stt])
            var4 = mv[:, 0:2 * NC].rearrange("p (n two) -> p n two", two=2)[:, :, 1]
            A(rs4[:, 0:NC], var4, AF.Ln, [B_stt], [B_stt], bias=EPS)
            A(rs4[:, 0:NC], rs4[:, 0:NC], AF.Exp, [B_stt], [B_stt], scale=-0.5)
            for n in range(NC):
                gv = S.gv[:, n, :]
                TS(dve, gv, gv, mv[:, 2 * n:2 * n + 1], rs4[:, n:n + 1], ALU.subtract, ALU.mult,
                   [S.B_gv[n], B_stt], [S.B_gv[n]])
                TT(dve, gv, gv, lng, ALU.mult, [S.B_gv[n], B_ln], [S.B_gv[n]])
                TT(dve, S.vtm[:, n, :], gv, lnb, ALU.add, [S.B_gv[n], B_ln], [S.B_vtm[n]])
                yield
            for h in range(4):
                ps, pb = bank()

                def sp_fn(e, ps=ps, h=h):
                    ins = None
                    for n in range(NC):
                        e.matmul(ps[:, n * 128:(n + 1) * 128], lhsT=S.vtm[:, n, h * 128:(h + 1) * 128], rhs=wsm[:, h, :],
                                 start=True, stop=False)
                        ins = e.matmul(ps[:, n * 128:(n + 1) * 128], lhsT=ones1[0:1, :],
                                       rhs=bsrow[0:1, h * 128:(h + 1) * 128], start=False, stop=True)
                    return ins
                kb.op(pe, sp_fn, S.B_vtm + [B_wsm, B_bs, B_const], [pb], cost=0.45 * NC, tag='spatial')
                TT(dve, gu[:, h, :], gu[:, h, :], ps[:, 0:N], ALU.mult, [B_gu[h], pb], [B_gu[h]])
                if h % FINE == FINE - 1:
                    yield
            yield from lru_b1(S)
            yield from lru_b2(S)
            yield from lru_scan(S, 0, True)
            yield from rms_stats(S, lambda c: gu[:, c, :], B_gu, 4, ones512, sqt, B_sq[0:4], cb=B_c512,
                                 src_all=gu, sq_all=sq[:, 0:4, :])
            r1, rb1 = S.last_r
            yield from rms_stats(S, lambda c: hs[:, c, :], B_hs, 4, ones512, lambda c: sq[:, 4 + c, :], B_sq[4:8],
                                 cb=B_c512, src_all=hs, sq_all=sq[:, 4:8, :])
            r2, rb2 = S.last_r
            yield from rms_apply(lambda c: gu[:, c, :], B_gu, 4, 56, r1, rb1, lambda c: xn[:, c, :], B_xn[0:4])
            yield from rms_apply(lambda c: hs[:, c, :], B_hs, 4, 60, r2, rb2, lambda c: xn[:, 4 + c, :], B_xn[4:8])
            yield from proj_evac(S, w_out, [B_wout], lambda k: xn[:, k, :], B_xn, 8, yg, B_yg, sqt, B_sq)
            yield from postnorm_res(S, lambda oc: xt[:, oc, :], B_xt, yg, B_yg, sqt, B_sq)
            kb.dma(sp, lambda e: e.dma_start(
                out=h1d[:, i * N:(i + 1) * N].rearrange("(k p) n -> p k n", p=128), in_=xt), reads=B_xt,
                writes=[B_h1d[i]], cost=8.0)
            yield

        run_pipelined(mix_tile, NMT, LAG_M, k=3)

        wload(w_q, w_q_d, [B_wq])
        wload(wkvK, w_kv_d[:, 0:1024], [B_wkvK])
        wload(wkvV, w_kv_d[:, 1024:2048], [B_wkvV])
        wload(w_o, w_o_d, [B_wo])
        kb.dma(sp, lambda e: e.dma_start(out=mt, in_=memT.rearrange("(k p) n -> p k n", p=128)), writes=B_mt)
        for _ in rms_stats(KS, lambda c: mt[:, c, :], B_mt, KC, ones1024, lambda c: msq[:, c, :], B_msq):
            pass
        r, rb = KS.last_r
        for _ in rms_apply(lambda c: mt[:, c, :], B_mt, KC, 48, r, rb, lambda c: mn[:, c, :], B_mn):
            pass
        for dc in range(8):
            ps, pb = bank()
            mm_group(ps[:, 0:256], [(wkvK[:, k, dc * 128:(dc + 1) * 128], mn[:, k, :]) for k in range(KC)],
                     [B_wkvK] + B_mn, [pb])
            A(kT[:, dc, :], ps[:, 0:256], AF.Copy, [pb], [B_kT])
        for mc in range(2):
            for hf in range(2):
                ps, pb = bank()
                mm_group(ps[:, :], [(mn[:, k, mc * 128:(mc + 1) * 128], wkvV[:, k, hf * 512:(hf + 1) * 512])
                                    for k in range(KC)], [B_wkvV] + B_mn, [pb])
                A(vv[:, mc, hf * 512:(hf + 1) * 512], ps[:, :], AF.Copy, [pb], [B_vv])

        def att_tile(i):
            S = AS[i % 3]
            N = S.N
            xt, xn, sq = S.xt, S.xn, S.sq
            B_xt, B_xn, B_sq = S.B_xt, S.B_xn, S.B_sq
            yg = lambda oc: S.yg[:, oc, :]
            B_yg = S.B_yg
            sqt = lambda c: sq[:, c, :]
            kb.dma(sp, lambda e: e.dma_start(
                out=xt, in_=h1d[:, i * N:(i + 1) * N].rearrange("(k p) n -> p k n", p=128)), reads=[B_h1d[i]],
                writes=B_xt, cost=8.0)
            yield
            yield from rms_stats(S, lambda c: xt[:, c, :], B_xt, KC, ones1024, sqt, B_sq, src_all=xt, sq_all=sq)
            r, rb = S.last_r
            yield from rms_apply(lambda c: xt[:, c, :], B_xt, KC, 16, r, rb, lambda c: xn[:, c, :], B_xn)
            for oc in range(KC):
                ps, pb = bank()
                mm_group(ps[:, 0:N], [(w_q[:, k, oc * 128:(oc + 1) * 128], xn[:, k, :]) for k in range(KC)],
                         [B_wq] + B_xn, [pb])
                A(S.qT[:, oc, :], ps[:, 0:N], AF.Copy, [pb], [S.B_qT[oc]])
                if oc % FINE == FINE - 1:
                    yield
            for h in range(4):
                for mc in range(2):
                    ps, pb = bank()
                    mm_group(ps[:, 0:N], [(kT[:, 2 * h + dd, mc * 128:(mc + 1) * 128], S.qT[:, 2 * h + dd, :])
                                          for dd in range(2)], [B_kT, S.B_qT[2 * h], S.B_qT[2 * h + 1]], [pb])
                    A(sq[:, 2 * h + mc, :], ps[:, 0:N], AF.Exp, [pb], [B_sq[2 * h + mc]], scale=1.0 / 16.0)
                ps, pb = bank()
                mm_group(ps[:, 0:N], [(ones1[:, :], sq[:, 2 * h + mc, :]) for mc in range(2)],
                         [B_const, B_sq[2 * h], B_sq[2 * h + 1]], [pb])
                ri = h % 2
                A(S.rc[ri], ps[:, 0:N], AF.Ln, [pb], S.B_rc[ri])
                A(S.rc[ri], S.rc[ri], AF.Exp, S.B_rc[ri], S.B_rc[ri], scale=-1.0)
                for dd in range(2):
                    ps, pb = bank()
                    mm_group(ps[:, 0:N], [(vv[:, mc, h * 256 + dd * 128:h * 256 + (dd + 1) * 128], sq[:, 2 * h + mc, :])
                                          for mc in range(2)], [B_vv, B_sq[2 * h], B_sq[2 * h + 1]], [pb])
                    TT(dve, xn[:, 2 * h + dd, :], ps[:, 0:N], S.rc[ri], ALU.mult, [pb] + S.B_rc[ri], [B_xn[2 * h + dd]])
                yield
            yield from proj_evac(S, w_o, [B_wo], lambda k: xn[:, k, :], B_xn, 24, yg, B_yg, sqt, B_sq)
            yield from postnorm_res(S, lambda oc: xt[:, oc, :], B_xt, yg, B_yg, sqt, B_sq)
            kb.dma(sp, lambda e: e.dma_start(
                out=h2d[:, i * N:(i + 1) * N].rearrange("(k p) n -> p k n", p=128), in_=xt), reads=B_xt,
                writes=[B_h2d[i]], cost=8.0)
            yield

        run_pipelined(att_tile, NMT, 8, k=3)

        ar.off = F_BASE
        half = DFF // 2
        FH = FC // 2
        wgA, bga_, _ = allocb([KC, half], BF16, 1, "wgA")
        wuA, bua_, _ = allocb([KC, half], BF16, 1, "wuA")
        wgB, bgb_, _ = allocb([KC, half], BF16, 1, "wgB")
        wuB, bub_, _ = allocb([KC, half], BF16, 1, "wuB")
        wd, bwd_, _ = allocb([FC, D], BF16, 2, "wd")
        wgH, wuH = [wgA, wgB], [wuA, wuB]
        B_wg, B_wu, B_wd = [bga_[0], bgb_[0]], [bua_[0], bub_[0]], bwd_
        h2t, B_h2t, fxn, B_fxn = [], [], [], []
        for _i in range(2):
            t_, b_, _ = allocb([KC, NF], F32, 8, "h2t")
            h2t.append(t_)
            B_h2t.append(b_)
        for _i in range(2):
            t_, b_, _ = allocb([KC, NF], BF16, 8, "fxn")
            fxn.append(t_)
            B_fxn.append(b_)
        fsqp, B_fsqp, _ = allocb([KC, NF], BF16, 8, "fsqp")
        fsq, B_fsq, _ = allocb([KC, NF], BF16, 8, "fsq")
        fact, B_fact, _ = allocb([FC, NF], BF16, FC, "fact")
        fyg, B_fyg, _ = allocb([KC, NF], F32, 8, "fyg")
        fsg, B_fsg = [], []
        for _i in range(2):
            t_, b_, _ = allocb([NF], F32, 1, "fsg")
            fsg.append(t_)
            B_fsg.append(b_[0])
        FS = Set()
        FS.N = NF
        t_, b_, _ = allocb([NF], F32, 1, "fmse")
        FS.mse, FS.B_mse = t_, b_[0]
        FS.rstd, FS.B_rstd = [], []
        for _i in range(2):
            t_, b_, _ = allocb([NF], F32, 1, "frstd")
            FS.rstd.append(t_)
            FS.B_rstd.append(b_[0])
        FS.rr = 0

        def fload(dst, src_, wr, cost):
            kb.dma(pool, lambda e: e.dma_start(out=dst, in_=src_.rearrange("(k p) c -> p k c", p=128)), writes=wr,
                   cost=cost)
        fload(wgA, w_gate_d[:, 0:half], [B_wg[0]], 26.0)
        fload(wuA, w_up_d[:, 0:half], [B_wu[0]], 26.0)
        fload(wgB, w_gate_d[:, half:DFF], [B_wg[1]], 26.0)
        fload(wuB, w_up_d[:, half:DFF], [B_wu[1]], 26.0)
        fload(wd[:, 0:FH, :], w_down_d[0:FH * 128, :], [B_wd[0]], 26.0)
        fload(wd[:, FH:FC, :], w_down_d[FH * 128:DFF, :], [B_wd[1]], 26.0)
        out_tokens = []
        NFT = SEG // NF

        def f_load(t):
            hh, B_hh = h2t[t % 2], B_h2t[t % 2]
            kb.dma(sp, lambda e: e.dma_start(
                out=hh, in_=h2d[:, t * NF:(t + 1) * NF].rearrange("(k p) n -> p k n", p=128)), reads=[B_h2d[t]],
                writes=B_hh, cost=8.0)

        def f_prenorm(t):
            hh, B_hh = h2t[t % 2], B_h2t[t % 2]
            for _ in rms_stats(FS, lambda c: hh[:, c, :], B_hh, KC, ones1024, lambda c: fsqp[:, c, :], B_fsqp,
                               src_all=hh, sq_all=fsqp):
                pass
            r_, rb_ = FS.last_r
            for _ in rms_apply(lambda c: hh[:, c, :], B_hh, KC, 32, r_, rb_, lambda c: fxn[t % 2][:, c, :], B_fxn[t % 2]):
                pass

        f_load(0)
        f_prenorm(0)
        for t in range(NFT):
            hh, B_hh = h2t[t % 2], B_h2t[t % 2]
            xn_t, B_xn_t = fxn[t % 2], B_fxn[t % 2]
            for f in range(FC):
                psg, pbg = bank()
                hf_, fo_ = f // FH, (f % FH) * 128
                mm_group(psg[:, 0:NF], [(wgH[hf_][:, k, fo_:fo_ + 128], xn_t[:, k, :]) for k in range(KC)],
                         [B_wg[hf_]] + B_xn_t, [pbg])
                psu, pbu = bank()
                mm_group(psu[:, 0:NF], [(wuH[hf_][:, k, fo_:fo_ + 128], xn_t[:, k, :]) for k in range(KC)],
                         [B_wu[hf_]] + B_xn_t, [pbu])
                si = f % 2
                A(fsg[si], psg[:, 0:NF], AF.Silu, [pbg], [B_fsg[si]])
                TT(dve, fact[:, f, :], fsg[si], psu[:, 0:NF], ALU.mult, [B_fsg[si], pbu], [B_fact[f]])
                if t + 1 < NFT and f == 3:
                    f_load(t + 1)
                if t + 1 < NFT and f == 9:
                    f_prenorm(t + 1)
            for _ in proj_evac(FS, wd, B_wd, lambda k: fact[:, k, :], B_fact, 40,
                               lambda oc: fyg[:, oc, :], B_fyg, lambda c: fsq[:, c, :], B_fsq, nk=FC):
                pass
            for _ in postnorm_res(FS, lambda oc: hh[:, oc, :], B_hh, lambda oc: fyg[:, oc, :], B_fyg,
                                  lambda c: fsq[:, c, :], B_fsq):
                pass
            tok = kb.dma(sp, lambda e, t=t, hh=hh: e.dma_start(
                out=outT[:, t * NF:(t + 1) * NF].rearrange("(k p) n -> p k n", p=128), in_=hh), reads=B_hh, cost=8.0)
            out_tokens.append(tok)
        kb.join(sp, out_tokens)
        with nc.Block() as block:
            kb.emit(block)
    nc._kb = kb
    return nc


def _colvec(v, nchunk):
    return np.ascontiguousarray(np.asarray(v, np.float32).reshape(nchunk, 128).T)


def kernel(x, mem, w_in, ln_v_g, ln_v_b, w_s, b_s, conv_w, conv_b, w_a, b_a, w_x, b_x, lam,
           g_out_gmlp, g_out_lru, w_out, w_q, w_kv, w_o, w_gate, w_up, w_down,
           n_pre_mix, n_post_mix, n_pre_x, n_mem, n_post_x, n_pre_ffn, n_post_ffn):
    f = lambda a: np.asarray(a, np.float32)
    x, mem = f(x), f(mem)
    vecs = np.zeros((128, NV), np.float32)
    for col, v in ((0, n_pre_mix), (8, n_post_mix), (16, n_pre_x), (24, n_post_x), (32, n_pre_ffn),
                   (40, n_post_ffn), (48, n_mem)):
        vecs[:, col:col + 8] = _colvec(f(v)[0], 8)
    vecs[:, 56:60] = _colvec(f(g_out_gmlp)[0], 4)
    vecs[:, 60:64] = _colvec(f(g_out_lru)[0], 4)
    for k in range(4):
        vecs[:, 64 + 4 * k:68 + 4 * k] = _colvec(f(conv_w)[0, k], 4)
    vecs[:, 80:84] = _colvec(f(conv_b)[0], 4)
    vecs[:, 84:88] = _colvec(f(b_a)[0], 4)
    vecs[:, 88:92] = _colvec(f(b_x)[0], 4)
    vecs[:, 92:96] = _colvec(f(lam)[0], 4)
    lnrow = np.ascontiguousarray(np.broadcast_to(
        np.concatenate([f(ln_v_g)[0], f(ln_v_b)[0]])[None, :], (128, 1024)))
    wsT = np.ascontiguousarray(f(w_s)[0].transpose(2, 0, 1).reshape(128, 512))
    cmask = np.triu(np.ones((128, 128), np.float32))

    def blockdiag(w):
        w = f(w)[0]
        out = np.zeros((128, 4, 128), np.float32)
        for c in range(4):
            for hh in range(2):
                out[hh * 64:(hh + 1) * 64, c, hh * 64:(hh + 1) * 64] = w[2 * c + hh]
        return out.reshape(128, 512)
    wab, wxb = blockdiag(w_a), blockdiag(w_x)
    bsrow = np.ascontiguousarray(f(b_s)[0].reshape(1, 512))
    common = {
        "w_in": np.ascontiguousarray(f(w_in)[0]), "w_out": np.ascontiguousarray(f(w_out)[0]),
        "w_q": np.ascontiguousarray(f(w_q)[0]), "w_kv": np.ascontiguousarray(f(w_kv)[0]),
        "w_o": np.ascontiguousarray(f(w_o)[0]), "w_gate": np.ascontiguousarray(f(w_gate)[0]),
        "w_up": np.ascontiguousarray(f(w_up)[0]), "w_down": np.ascontiguousarray(f(w_down)[0]),
        "vecs": vecs, "lnrow": lnrow, "wsT": wsT, "cmask": cmask, "wab": wab, "wxb": wxb, "bsrow": bsrow,
    }
    in_maps = []
    for c in range(NCORES):
        b, j = c // 4, c % 4
        xsT = np.zeros((D, SEQ), np.float32)
        npre = SEG * j
        xsT[:, SEQ - SEG - npre:] = x[b, 0:npre + SEG, :].T
        tm = np.zeros((128, N_PRE), np.float32)
        tm[:, N_PRE - 4 * j:] = 1.0
        m = dict(common)
        m["xsT"] = xsT
        m["memT"] = np.ascontiguousarray(mem[b].T)
        m["tmask"] = tm
        in_maps.append(m)
    nc = build_program()
    res = run_bass_kernel_spmd(nc, in_maps, core_ids=list(range(NCORES)))
    out = np.empty((2, SEQ, D), np.float32)
    for c in range(NCORES):
        b, j = c // 4, c % 4
        out[b, j * SEG:(j + 1) * SEG, :] = res.results[c]["outT"].T
    return out
```
